# Optimizing a Trainium2 kernel written in Bass

```python
import math
import jax, jax.numpy as jnp
from jax import lax
import numpy as np

D_MODEL = 4096
BATCH = 2
SEQ = 4096
DEPTH = 2

HEAD_DIM = 128
A_HEADS = 8
A_WIDTH = A_HEADS * HEAD_DIM
A_SUB = HEAD_DIM // 2
B_GROUPS = ((128, 1), (512, 4), (2048, 16))
B_HEADS_PER_GROUP = 3
B_HEADS = B_HEADS_PER_GROUP * len(B_GROUPS)
B_WIDTH = B_HEADS * HEAD_DIM
C_WINDOWS = (2, 4, 8, 16)
C_GROUPS = len(C_WINDOWS)
C_WIDTH = 1024
C_GROUP_DIM = C_WIDTH // C_GROUPS
D_WIDTH = D_MODEL - A_WIDTH - B_WIDTH - C_WIDTH
CONV_WIDTH = 3
IN_SIZES = (A_WIDTH, A_WIDTH, A_WIDTH, B_WIDTH, B_WIDTH, B_WIDTH, C_WIDTH, D_WIDTH, D_WIDTH, D_WIDTH)
IN_WIDTH = sum(IN_SIZES)
IN_SPLITS = [int(v) for v in np.cumsum(IN_SIZES)[:-1]]
D_FF = 4 * D_MODEL
ROPE_THETA = 10000.0
Q_BLOCK = 128
NORM_EPS = 1e-6
DIFF_EPS = 1e-5

kernel_name = "hybrid_parallel_heads_diffattn_dilated_pool_shortconv"


def rms_norm(x, g, eps=NORM_EPS):
    xf = x.astype(jnp.float32)
    y = xf * lax.rsqrt(jnp.mean(xf * xf, axis=-1, keepdims=True) + eps)
    return (y * g.astype(jnp.float32)).astype(x.dtype)


def rope_tables(seq, dim):
    inv = ROPE_THETA ** (-jnp.arange(0, dim, 2, dtype=jnp.float32) / dim)
    ang = jnp.arange(seq, dtype=jnp.float32)[:, None] * inv[None, :]
    return jnp.cos(ang), jnp.sin(ang)


def apply_rope(x, cos, sin):
    shape = (1, x.shape[1]) + (1,) * (x.ndim - 3) + (cos.shape[-1],)
    c = cos.reshape(shape)
    s = sin.reshape(shape)
    xf = x.astype(jnp.float32)
    x1, x2 = jnp.split(xf, 2, axis=-1)
    out = jnp.concatenate([x1 * c - x2 * s, x1 * s + x2 * c], axis=-1)
    return out.astype(x.dtype)


def diff_attention(q, k, v, lam, subln_g, lam_init):
    b, s, h = q.shape[:3]
    nb = s // Q_BLOCK
    scale = A_SUB ** -0.5
    qb = q.reshape(b, nb, Q_BLOCK, h, 2, A_SUB).swapaxes(0, 1)
    kpos = jnp.arange(s)

    def block(args):
        qi, start = args
        sc = jnp.einsum('bqhcd,bkhcd->bchqk', qi, k, preferred_element_type=jnp.float32) * scale
        qpos = start + jnp.arange(Q_BLOCK)
        mask = qpos[:, None] >= kpos[None, :]
        p = jax.nn.softmax(jnp.where(mask, sc, -jnp.inf), axis=-1)
        w = p[:, 0] - lam * p[:, 1]
        return jnp.einsum('bhqk,bkhd->bqhd', w.astype(v.dtype), v)

    o = lax.map(block, (qb, jnp.arange(nb) * Q_BLOCK))
    o = o.swapaxes(0, 1).reshape(b, s, h, HEAD_DIM)
    o = rms_norm(o, subln_g, DIFF_EPS) * (1.0 - lam_init)
    return o.reshape(b, s, h * HEAD_DIM)


def dilated_group_attention(q, k, v, window, dilation):
    b, s, h, dh = q.shape
    n_keys = window // dilation + 1
    nb = s // Q_BLOCK
    offs = dilation * jnp.arange(n_keys)
    scale = dh ** -0.5
    qb = q.reshape(b, nb, Q_BLOCK, h, dh).swapaxes(0, 1)

    def block(args):
        qi, start = args
        idx = (start + jnp.arange(Q_BLOCK))[:, None] - offs[None, :]
        valid = idx >= 0
        idxc = jnp.maximum(idx, 0)
        kg = k[:, idxc]
        vg = v[:, idxc]
        sc = jnp.einsum('bqhd,bqjhd->bhqj', qi, kg, preferred_element_type=jnp.float32) * scale
        sc = jnp.where(valid[None, None], sc, -jnp.inf)
        lse = jax.nn.logsumexp(sc, axis=-1, keepdims=True)
        p = jnp.exp(sc - lse)
        o = jnp.einsum('bhqj,bqjhd->bqhd', p.astype(v.dtype), vg)
        return o, lse[..., 0].transpose(0, 2, 1)

    o, lse = lax.map(block, (qb, jnp.arange(nb) * Q_BLOCK))
    o = o.swapaxes(0, 1).reshape(b, s, h, dh)
    lse = lse.swapaxes(0, 1).reshape(b, s, h)
    return o, lse


def dilated_mixture(q, k, v):
    b, s = q.shape[:2]
    n_g = len(B_GROUPS)
    q = q.reshape(b, s, n_g, B_HEADS_PER_GROUP, HEAD_DIM)
    k = k.reshape(b, s, n_g, B_HEADS_PER_GROUP, HEAD_DIM)
    v = v.reshape(b, s, n_g, B_HEADS_PER_GROUP, HEAD_DIM)
    outs, lses = [], []
    for g, (window, dilation) in enumerate(B_GROUPS):
        o, l = dilated_group_attention(q[:, :, g], k[:, :, g], v[:, :, g], window, dilation)
        outs.append(o)
        lses.append(l)
    o = jnp.stack(outs, axis=2)
    alpha = jax.nn.softmax(jnp.stack(lses, axis=2), axis=2)
    o = (o.astype(jnp.float32) * alpha[..., None]).astype(q.dtype)
    return o.reshape(b, s, B_WIDTH)


def multiscale_pool(u, pool_w, pool_scale):
    b, s, _ = u.shape
    uf = u.astype(jnp.float32).reshape(b, s, C_GROUPS, C_GROUP_DIM)
    cs = jnp.concatenate([jnp.zeros((b, 1, C_GROUPS, C_GROUP_DIM), jnp.float32),
                          jnp.cumsum(uf, axis=1)], axis=1)
    t = jnp.arange(s)[:, None]
    win = jnp.array(C_WINDOWS, dtype=jnp.int32)[None, :]
    lo = jnp.maximum(t + 1 - win, 0)
    cnt = jnp.minimum(t + 1, win).astype(jnp.float32)
    gidx = jnp.arange(C_GROUPS)[None, :]
    mean = (cs[:, 1:] - cs[:, lo, gidx]) / cnt[None, :, :, None]
    pooled = mean - uf
    y = jnp.einsum('bsgc,gcd->bsgd', pooled, pool_w.astype(jnp.float32))
    y = y.reshape(b, s, C_WIDTH) * pool_scale.astype(jnp.float32)
    return y.astype(u.dtype)


def short_conv_mixer(gate_b, gate_c, hx, conv_w):
    z = gate_c * hx
    y = lax.conv_general_dilated(z, conv_w[:, None, :].astype(z.dtype), window_strides=(1,),
                                 padding=[(CONV_WIDTH - 1, 0)],
                                 dimension_numbers=('NWC', 'WIO', 'NWC'),
                                 feature_group_count=z.shape[-1])
    return gate_b * y


def setup_inputs(seed: int = 0) -> dict:
    key = jax.random.key(seed)
    ks = jax.random.split(key, 14)
    f32 = jnp.float32
    nrm = lambda k, shape, sc: jax.random.normal(k, shape, f32) * sc
    return {
        "x": nrm(ks[0], (BATCH, SEQ, D_MODEL), 1.0),
        "w_in": nrm(ks[1], (DEPTH, D_MODEL, IN_WIDTH), D_MODEL ** -0.5),
        "w_out": nrm(ks[2], (DEPTH, D_MODEL, D_MODEL), D_MODEL ** -0.5),
        "norm_mix": 1.0 + nrm(ks[3], (DEPTH, D_MODEL), 0.02),
        "norm_mlp": 1.0 + nrm(ks[4], (DEPTH, D_MODEL), 0.02),
        "diff_lambda": nrm(ks[5], (DEPTH, 4, A_SUB), 0.1),
        "diff_subln": 1.0 + nrm(ks[6], (DEPTH, HEAD_DIM), 0.02),
        "pool_w": nrm(ks[7], (DEPTH, C_GROUPS, C_GROUP_DIM, C_GROUP_DIM), C_GROUP_DIM ** -0.5),
        "pool_scale": 1.0 + nrm(ks[8], (DEPTH, C_WIDTH), 0.1),
        "conv_w": nrm(ks[9], (DEPTH, CONV_WIDTH, D_WIDTH), CONV_WIDTH ** -0.5),
        "w_up": nrm(ks[10], (DEPTH, D_MODEL, D_FF), D_MODEL ** -0.5),
        "w_down": nrm(ks[11], (DEPTH, D_FF, D_MODEL), D_FF ** -0.5),
        "norm_final": 1.0 + nrm(ks[12], (D_MODEL,), 0.02),
    }


def reference(x, w_in, w_out, norm_mix, norm_mlp, diff_lambda, diff_subln, pool_w, pool_scale,
              conv_w, w_up, w_down, norm_final):
    b, s, _ = x.shape
    cos_a, sin_a = rope_tables(s, A_SUB)
    cos_b, sin_b = rope_tables(s, HEAD_DIM)
    for l in range(DEPTH):
        h = rms_norm(x, norm_mix[l])
        proj = jnp.einsum('bsd,de->bse', h, w_in[l])
        qa, ka, va, qb, kb, vb, u, gate_b, gate_c, hd = jnp.split(proj, IN_SPLITS, axis=-1)

        lam_init = 0.8 - 0.6 * math.exp(-0.3 * l)
        lp = diff_lambda[l].astype(jnp.float32)
        lam = jnp.exp(jnp.sum(lp[0] * lp[1])) - jnp.exp(jnp.sum(lp[2] * lp[3])) + lam_init
        qa = apply_rope(qa.reshape(b, s, A_HEADS, 2, A_SUB), cos_a, sin_a)
        ka = apply_rope(ka.reshape(b, s, A_HEADS, 2, A_SUB), cos_a, sin_a)
        va = va.reshape(b, s, A_HEADS, HEAD_DIM)
        out_a = diff_attention(qa, ka, va, lam, diff_subln[l], lam_init)

        qb = apply_rope(qb.reshape(b, s, B_HEADS, HEAD_DIM), cos_b, sin_b)
        kb = apply_rope(kb.reshape(b, s, B_HEADS, HEAD_DIM), cos_b, sin_b)
        vb = vb.reshape(b, s, B_HEADS, HEAD_DIM)
        out_b = dilated_mixture(qb, kb, vb)

        out_c = multiscale_pool(u, pool_w[l], pool_scale[l])

        out_d = short_conv_mixer(gate_b, gate_c, hd, conv_w[l])

        mix = jnp.concatenate([out_a, out_b, out_c, out_d], axis=-1)
        x = x + jnp.einsum('bse,ed->bsd', mix, w_out[l])

        h = rms_norm(x, norm_mlp[l])
        act = jnp.square(jax.nn.relu(jnp.einsum('bsd,df->bsf', h, w_up[l])))
        x = x + jnp.einsum('bsf,fd->bsd', act, w_down[l])
    return rms_norm(x, norm_final)
```

```python
import math
import numpy as np
from contextlib import ExitStack
import concourse.bass as bass
import concourse.mybir as mybir
from concourse.bass_utils import run_bass_kernel_spmd

F32 = mybir.dt.float32
BF16 = mybir.dt.bfloat16
ALU = mybir.AluOpType
AF = mybir.ActivationFunctionType
AX = mybir.AxisListType

D = 4096
T = 1024
KC = 32
NCORE = 8
INW = 10240
NORM_EPS = 1e-6
DIFF_EPS = 1e-5
B_DIL = (1, 4, 16)
C_WIN = (2, 4, 8, 16)
XL_ROWS = 4352
INPROJ_NB = 40
SKIP = set()
HC = 160


class Cfg:
    def __init__(self, depth=2, d_ff=16384):
        self.L = depth
        self.DFF = d_ff
        self.FG = d_ff // 4
        self.KCG = self.FG // 128
        self.NBU = d_ff // 256
        assert self.NBU % 8 == 0


class _Op:
    __slots__ = ("eng", "fn", "deps", "kind", "sig", "cnt", "dsem", "dval", "prev")


class Prog:
    ENGS = ("pe", "act", "dve", "pool", "sp")
    NDS = 8

    def __init__(self):
        self.ops = []
        self.lastw = {}
        self.rd_c = {}
        self.rd_d = {}
        self.bar = []
        self.last_on = {e: None for e in self.ENGS}
        self.dma_hist = {e: [] for e in self.ENGS}
        self.last_cc = None
        self.last_cc_fg = None
        self.dma_fg = {e: [] for e in self.ENGS}
        self.ncc = 0

    @staticmethod
    def _is_psum(b):
        return b in ("pn", "psT") or (isinstance(b, tuple) and len(b) == 2 and b[0] == "ps")

    def _add(self, eng, fn, reads, writes, kind, bg=False):
        pr = [b for b in reads if self._is_psum(b)]
        if pr:
            writes = list(writes) + pr
            reads = [b for b in reads if not self._is_psum(b)]
        deps = set(self.bar)
        for b in reads:
            w = self.lastw.get(b)
            if w is not None:
                deps.add(w)
        for b in writes:
            w = self.lastw.get(b)
            if w is not None:
                deps.add(w)
            r = self.rd_c.get(b)
            if r:
                deps.update(r.values())
            r = self.rd_d.get(b)
            if r:
                deps.update(r)
        if kind == "cc" and self.last_cc is not None:
            deps.add(self.last_cc)
        i = len(self.ops)
        o = _Op()
        o.eng, o.fn, o.deps, o.kind, o.sig, o.cnt = eng, fn, deps, kind, False, 0
        o.dsem = o.dval = o.prev = None
        self.ops.append(o)
        for b in reads:
            if kind == "c":
                self.rd_c.setdefault(b, {})[eng] = i
            else:
                self.rd_d.setdefault(b, []).append(i)
        for b in writes:
            self.lastw[b] = i
            self.rd_c[b] = {}
            self.rd_d[b] = []
        if not bg:
            self.last_on[eng] = i
        if kind == "dma":
            k = len(self.dma_hist[eng])
            o.dsem = (eng, k % self.NDS)
            o.dval = 16 * (k // self.NDS + 1)
            if k >= self.NDS:
                o.prev = (o.dsem, 16 * (k // self.NDS))
            self.dma_hist[eng].append(i)
            if not bg:
                self.dma_fg[eng].append(i)
        if kind == "cc":
            self.ncc += 1
            o.dsem = ("cc", 0)
            o.dval = self.ncc
            self.last_cc = i
            if not bg:
                self.last_cc_fg = i
        return i

    def op(self, eng, fn, reads=(), writes=()):
        return self._add(eng, fn, reads, writes, "c")

    def dma(self, eng, out, in_, reads=(), writes=(), bg=False, **kw):
        return self._add(eng, lambda e: e.dma_start(out=out, in_=in_, **kw), reads, writes, "dma", bg)

    def cc(self, fn, reads=(), writes=(), bg=False):
        return self._add("pool", fn, reads, writes, "cc", bg)

    def barrier(self):
        b = [v for v in self.last_on.values() if v is not None]
        for e in self.ENGS:
            b += self.dma_fg[e][-self.NDS:]
        if self.last_cc_fg is not None:
            b.append(self.last_cc_fg)
        self.bar = b

    def emit(self, nc, es):
        ops = self.ops
        for x in ops:
            for d in x.deps:
                y = ops[d]
                if y.kind == "c" and not (x.kind == "c" and x.eng == y.eng == "pe"):
                    y.sig = True
        cnt = {e: 0 for e in self.ENGS}
        for o in ops:
            if o.kind == "c" and o.sig:
                cnt[o.eng] += 1
                o.cnt = cnt[o.eng]
        sem = {}
        for e in self.ENGS:
            sem[("c", e)] = es.enter_context(nc.semaphore("c_" + e))
            if self.dma_hist[e]:
                for k in range(self.NDS):
                    sem[(e, k)] = es.enter_context(nc.semaphore("d_%s%d" % (e, k)))
        sem[("cc", 0)] = es.enter_context(nc.semaphore("ccs"))
        per = {e: [] for e in self.ENGS}
        for o in ops:
            per[o.eng].append(o)
        block = es.enter_context(nc.Block())

        def run(engname, eng):
            known = {}
            for o in per[engname]:
                waits = {}
                for d in o.deps:
                    y = ops[d]
                    if y.kind == "c":
                        if o.kind == "c" and y.eng == engname == "pe":
                            continue
                        key, val = ("c", y.eng), y.cnt
                    else:
                        key, val = y.dsem, y.dval
                    if waits.get(key, 0) < val:
                        waits[key] = val
                if o.prev is not None:
                    key, val = o.prev
                    if waits.get(key, 0) < val:
                        waits[key] = val
                for key, val in waits.items():
                    if known.get(key, 0) < val:
                        eng.wait_ge(sem[key], val)
                        known[key] = val
                ins = o.fn(eng)
                if o.kind == "c":
                    if o.sig:
                        ins.then_inc(sem[("c", engname)], 1)
                elif o.kind == "dma":
                    ins.then_inc(sem[o.dsem], 16)
                else:
                    ins.then_inc(sem[o.dsem], 1)

        @block.tensor
        def _(e):
            run("pe", e)

        @block.scalar
        def _(e):
            run("act", e)

        @block.vector
        def _(e):
            run("dve", e)

        @block.gpsimd
        def _(e):
            run("pool", e)

        @block.sync
        def _(e):
            run("sp", e)


class Region:
    def __init__(self, ap_f32, nbytes):
        self.ap = ap_f32
        self.n = nbytes
        self.off = 0

    def reset(self):
        self.off = 0

    def alloc(self, shape, dt):
        es = 4 if dt == F32 else 2
        n = 1
        for s in shape:
            n *= s
        nb = (n * es + 31) // 32 * 32
        assert self.off + nb <= self.n, ("scratch overflow", self.off, nb, self.n)
        v = self.ap[:, self.off // 4:(self.off + nb) // 4]
        if dt != F32:
            v = v.bitcast(dt)
        v = v[:, 0:n]
        self.off += nb
        if len(shape) == 2:
            v = v.rearrange("p (a b) -> p a b", a=shape[0])
        elif len(shape) == 3:
            v = v.rearrange("p (a b c) -> p a b c", a=shape[0], b=shape[1])
        return v


def _prm_map(L):
    m = {}
    o = 0

    def put(name, n):
        nonlocal o
        m[name] = (o, n)
        o += n
    put("g_mix", L * 32)
    put("g_mlp", L * 32)
    put("g_fin", 32)
    put("subln", L)
    put("pscale", L * 8)
    put("convw", L * 21)
    put("lam", L * 256)
    put("thA", 64)
    put("hiB", 192)
    put("oh", 4)
    put("icnt", 64)
    put("RmA", 128)
    put("RmB", 128)
    put("ident", 128)
    return m, o


def up_local_to_global(NBU, r, bl):
    nbu = NBU // 8
    if NBU // 4 >= 8:
        q = NBU // 32
        g, j = bl // q, bl % q
        return g * (NBU // 4) + r * q + j
    return r * nbu + bl


def up_global_to_local(NBU, gb):
    nbu = NBU // 8
    if NBU // 4 >= 8:
        q = NBU // 32
        g, rem = gb // (NBU // 4), gb % (NBU // 4)
        return rem // q, g * q + rem % q
    return gb // nbu, gb % nbu


def seg_of(m):
    if m < 8:
        return ("qA", m)
    if m < 16:
        return ("kA", m - 8)
    if m < 24:
        return ("vA", m - 16)
    if m < 33:
        return ("qB", m - 24)
    if m < 42:
        return ("kB", m - 33)
    if m < 51:
        return ("vB", m - 42)
    if m < 59:
        return ("u", m - 51)
    if m < 66:
        return ("gb", m - 59)
    if m < 73:
        return ("gc", m - 66)
    return ("hd", m - 73)


def build(cfg, debug=False, stop=None):
    L, KCG, NBU = cfg.L, cfg.KCG, cfg.NBU
    nc = bass.Bass("TRN2", target_bir_lowering=False)
    P = Prog()
    es = ExitStack()
    PM, NPRM = _prm_map(L)

    def ext_in(name, shape, dt=F32):
        return nc.dram_tensor(name, shape, dt, kind="ExternalInput").ap()
    xT_in = ext_in("xT", [D, T])
    w_in_s = ext_in("w_in_s", [L * 5 * 128, 8192])
    w_out_s = ext_in("w_out_s", [L * 2 * 128, 8192])
    w_up_s = ext_in("w_up_s", [L * (NBU // 8) * 128, 8192])
    w_dn_s = ext_in("w_dn_s", [L * 8 * 128, KCG * 256])
    prm_in = ext_in("prm", [128, NPRM])
    rope_in = ext_in("rope", [128, 4 * T])
    rtab_in = ext_in("rtab", [128, 5 * 512])
    poolw_in = ext_in("poolw", [128, L * 2048])
    outT = nc.dram_tensor("outT", [D, T], F32, kind="ExternalOutput").ap()

    def internal(name, shape, dt):
        return nc.dram_tensor(name, shape, dt)
    wshapes = {"in": (5, 8192), "out": (2, 8192), "up": (NBU // 8, 8192), "dn": (8, KCG * 256)}
    wsrc = {"in": w_in_s, "out": w_out_s, "up": w_up_s, "dn": w_dn_s}
    wl, w4, wa = {}, {}, {}
    for l in range(L):
        for nm, (nb, cols) in wshapes.items():
            qc = cols // 4
            for bl in range(nb):
                wl[(nm, l, bl)] = internal("wl_%s%d_%d" % (nm, l, bl), [4, 128, qc], BF16)
                for q in range(4):
                    w4[(nm, l, bl, q)] = internal("w4_%s%d_%d_%d" % (nm, l, bl, q), [4 * 128, qc], BF16)
                    wa[(nm, l, bl, q)] = internal("wa_%s%d_%d_%d" % (nm, l, bl, q), [8 * 128, qc], BF16)

    def w_tile(nm, l, r, bl):
        nb, cols = wshapes[nm]
        qc = cols // 4
        pieces = [(wa[(nm, l, bl, q)][r * 128:(r + 1) * 128, :], q * qc, qc, ("wa", nm, l, bl, q)) for q in range(4)]
        return pieces, cols

    xres = internal("xres", [KC, 128, T], F32)
    qsA = internal("qsA", [8, 128, T], BF16)
    qsB = internal("qsB", [9, 128, T], BF16)
    fsc = {k: internal("fs_" + k, [n, 128, T], F32) for k, n in (("u", 8), ("gb", 7), ("gc", 7), ("hd", 7))}
    XU = [("kA", h) for h in range(8)] + [("kB", h) for h in range(9)] + [("vA", h) for h in range(8)] + [("vB", h) for h in range(9)]
    xl = {u: internal("xl_%s%d" % u, [128, 1024], BF16) for u in XU}
    xa = {u: internal("xa_%s%d" % u, [4 * 128, 1024], BF16) for u in XU}
    hl = internal("hl", [128, HC], F32)
    ha = internal("ha", [4 * 128, HC], F32)
    dbg = {}
    if debug:
        dbg["mix"] = nc.dram_tensor("dbg_mix", [KC, 128, T], BF16, kind="ExternalOutput").ap()
        dbg["h"] = nc.dram_tensor("dbg_h", [KC, 128, T], BF16, kind="ExternalOutput").ap()
        dbg["x1"] = nc.dram_tensor("dbg_x1", [KC, 128, T], F32, kind="ExternalOutput").ap()
        dbg["qA"] = nc.dram_tensor("dbg_qA", [8, 128, T], BF16, kind="ExternalOutput").ap()

    def sb(name, shape, dt):
        return es.enter_context(nc.sbuf_tensor(name, shape, dt))
    hT = sb("hT", [128, KC, T], BF16)
    NWB = 2
    wbuf = [sb("wb%d" % i, [128, 8192], BF16) for i in range(NWB)]
    prm = sb("prm_sb", [128, NPRM], F32)
    cbf = sb("cbf", [128, 4 * 128], BF16)
    lamv = sb("lamv", [128, 8 * L], F32)
    SCR_BYTES = 100 * 1024
    scr_t = sb("scr", [128, SCR_BYTES // 4], F32)
    S = Region(scr_t[:], SCR_BYTES)
    ps = [es.enter_context(nc.psum_tensor("ps%d" % i, [128, 512], F32))[:, :] for i in range(8)]
    psT = ps[7].bitcast(BF16)

    ones_bf = cbf[:, 0:128]
    ident_bf = cbf[:, 128:256]
    RmA_bf = cbf[:, 256:384]
    RmB_bf = cbf[:, 384:512]

    def pcol(name, idx=0, n=1):
        o, _ = PM[name]
        return prm[:, o + idx:o + idx + n]

    class WS:
        seq = []
        nload = 0
        nuse = 0

    def w_plan(nm, l):
        nb, cols = wshapes[nm]
        for gb in range(8 * nb):
            WS.seq.append(w_tile(nm, l, gb // nb, gb % nb))

    def w_prefetch():
        while WS.nload < len(WS.seq) and WS.nload < WS.nuse + NWB:
            pieces, cols = WS.seq[WS.nload]
            j = WS.nload % NWB
            for src, c0, qc, bid in pieces:
                P.dma("sp", wbuf[j][:, c0:c0 + qc], src, reads=[bid], writes=[("wb", j, c0)])
            WS.nload += 1

    def w_get():
        w_prefetch()
        j = WS.nuse % NWB
        pieces, cols = WS.seq[WS.nuse]
        WS.nuse += 1
        return wbuf[j], [("wb", j, c0) for _, c0, _, _ in pieces]

    for l in range(L):
        w_plan("in", l)
        w_plan("out", l)
        for g in range(4):
            nbu = NBU // 8
            for gb in range(g * (NBU // 4), (g + 1) * (NBU // 4)):
                r_, bl_ = up_global_to_local(NBU, gb)
                WS.seq.append(w_tile("up", l, r_, bl_))
            for cb in range(16):
                WS.seq.append(w_tile("dn", l, cb // 2, g * 2 + cb % 2))

    G4 = [[0, 1, 2, 3], [4, 5, 6, 7]]
    G2 = [[0, 4], [1, 5], [2, 6], [3, 7]]

    def cast_and_gather(nm, l, bls=None):
        nb, cols = wshapes[nm]
        qc = cols // 4
        for bl in (range(nb) if bls is None else bls):
            src = wsrc[nm][(l * nb + bl) * 128:(l * nb + bl + 1) * 128, :]
            P.dma("pool", wl[(nm, l, bl)].ap().rearrange("q p c -> p q c"), src.rearrange("p (q c) -> p q c", q=4),
                  reads=[], writes=[("wl", nm, l, bl)], bg=True)
            for q in range(4):
                k = (nm, l, bl, q)
                P.cc(lambda e, k=k, bl=bl, q=q: e.collective_compute("AllGather", ALU.bypass, replica_groups=G4,
                                                                   ins=[wl[k[:3]][q]], outs=[w4[k].ap()]),
                     reads=[("wl", nm, l, bl)], writes=[("w4",) + k], bg=True)
                P.cc(lambda e, k=k: e.collective_compute("AllGather", ALU.bypass, replica_groups=G2,
                                                       ins=[w4[k].ap()], outs=[wa[k].ap()]),
                     reads=[("w4",) + k], writes=[("wa",) + k], bg=True)

    P.dma("sp", prm[:], prm_in, writes=["prm"])
    for kc4 in range(0, KC, 8):
        P.dma("sp", xres[kc4:kc4 + 8].rearrange("k p t -> (k p) t"), xT_in[kc4 * 128:(kc4 + 8) * 128, :],
              writes=[("x", m, tb) for m in range(kc4, kc4 + 8) for tb in range(2)])
    cast_and_gather("in", 0)
    cast_and_gather("out", 0)
    P.op("dve", lambda e: e.memset(cbf[:, 0:128], 1.0), writes=["cbf"])
    P.op("act", lambda e: e.activation(out=cbf[:, 128:256], in_=prm[:, PM["ident"][0]:PM["ident"][0] + 128], func=AF.Copy),
         reads=["prm"], writes=["cbf"])
    P.op("act", lambda e: e.activation(out=cbf[:, 256:512], in_=prm[:, PM["RmA"][0]:PM["RmA"][0] + 256], func=AF.Copy),
         reads=["prm"], writes=["cbf"])
    for l in range(L):
        lam_init = 0.8 - 0.6 * math.exp(-0.3 * l)
        lo = PM["lam"][0] + l * 256
        c0 = 8 * l
        t12 = lamv[:, c0 + 4:c0 + 6]
        S.reset()
        pr = S.alloc([128], F32)
        for i in range(2):
            P.op("dve", lambda e, i=i, lo=lo: e.tensor_tensor(out=pr[:, 0:64], in0=prm[:, lo + 128 * i:lo + 128 * i + 64],
                                                            in1=prm[:, lo + 128 * i + 64:lo + 128 * i + 128], op=ALU.mult),
                 reads=["prm"], writes=["pr"])
            P.op("dve", lambda e, i=i, c0=c0: e.reduce_sum(out=lamv[:, c0 + 4 + i:c0 + 5 + i], in_=pr[:, 0:64], axis=AX.X),
                 reads=["pr"], writes=["lamv"])
        P.op("act", lambda e, t12=t12: e.activation(out=t12, in_=t12, func=AF.Exp), reads=["lamv"], writes=["lamv"])
        P.op("dve", lambda e, c0=c0: e.tensor_tensor(out=lamv[:, c0:c0 + 1], in0=lamv[:, c0 + 5:c0 + 6],
                                                    in1=lamv[:, c0 + 4:c0 + 5], op=ALU.subtract),
             reads=["lamv"], writes=["lamv"])
        P.op("dve", lambda e, c0=c0, li=lam_init: e.tensor_scalar(out=lamv[:, c0:c0 + 1], in0=lamv[:, c0:c0 + 1],
                                                                scalar1=-li, scalar2=None, op0=ALU.add),
             reads=["lamv"], writes=["lamv"])
        P.op("dve", lambda e, c0=c0, l=l, li=lam_init: e.tensor_scalar(out=lamv[:, c0 + 1:c0 + 2], in0=pcol("subln", l),
                                                                     scalar1=1.0 - li, scalar2=None, op0=ALU.mult),
             reads=["prm"], writes=["lamv"])
    P.barrier()

    def norm_phase(gname, l, final=False):
        S.reset()
        xs = [S.alloc([KC, 256], F32) for _ in range(2)]
        sq = [S.alloc([256], BF16) for _ in range(4)]
        rs = [S.alloc([256], F32) for _ in range(2)]
        ost = [S.alloc([256], F32) for _ in range(4)] if final else None
        pn = ps[6]
        nsq = 0
        for sbk in range(4):
            t0 = sbk * 256
            tb = sbk // 2
            X = xs[sbk % 2]
            xid = ("xs", sbk % 2)
            P.dma("sp", X, xres[:, :, t0:t0 + 256].rearrange("k p t -> p k t"),
                  reads=[("x", m, tb) for m in range(KC)], writes=[xid])
            for kc in range(KC):
                q = sq[nsq % 4]
                qid = ("sq", nsq % 4)
                nsq += 1
                P.op("act", lambda e, q=q, X=X, kc=kc: e.activation(out=q, in_=X[:, kc, :], func=AF.Square),
                     reads=[xid], writes=[qid])
                P.op("pe", lambda e, q=q, kc=kc: e.matmul(pn[:, 0:256], lhsT=ones_bf, rhs=q, start=(kc == 0), stop=(kc == KC - 1)),
                     reads=[qid, "cbf"], writes=["pn"])
            R_ = rs[sbk % 2]
            rid = ("rs", sbk % 2)
            P.op("dve", lambda e, R_=R_: e.tensor_scalar(out=R_, in0=pn[:, 0:256], scalar1=1.0 / D, scalar2=NORM_EPS,
                                                       op0=ALU.mult, op1=ALU.add), reads=["pn"], writes=[rid])
            P.op("act", lambda e, R_=R_: e.activation(out=R_, in_=R_, func=AF.Sqrt), reads=[rid], writes=[rid])
            P.op("dve", lambda e, R_=R_: e.reciprocal(out=R_, in_=R_), reads=[rid], writes=[rid])
            for kc in range(KC):
                gcol = pcol(gname, (l * 32 if gname != "g_fin" else 0) + kc)
                if not final:
                    P.op("dve", lambda e, X=X, kc=kc, gcol=gcol, R_=R_, t0=t0: e.scalar_tensor_tensor(
                        out=hT[:, kc, t0:t0 + 256], in0=X[:, kc, :], scalar=gcol, in1=R_, op0=ALU.mult, op1=ALU.mult),
                        reads=[xid, rid, "prm"], writes=[("hT", tb)])
                else:
                    o_ = ost[kc % 4]
                    oid = ("ost", kc % 4)
                    P.op("dve", lambda e, X=X, kc=kc, gcol=gcol, R_=R_, o_=o_: e.scalar_tensor_tensor(
                        out=o_, in0=X[:, kc, :], scalar=gcol, in1=R_, op0=ALU.mult, op1=ALU.mult),
                        reads=[xid, rid, "prm"], writes=[oid])
                    P.dma("sp", outT[kc * 128:(kc + 1) * 128, t0:t0 + 256], o_, reads=[oid], writes=[("out", kc, sbk)])
        P.barrier()

    class _View:
        def __init__(self, kind, j):
            self.kind, self.j = kind, j

        def __getitem__(self, idx):
            if isinstance(idx, tuple):
                h = idx[0]
                rest = idx[1:]
            else:
                h, rest = idx, None
            if self.j is None:
                a = xl[(self.kind, h)].ap()
            else:
                a = xa[(self.kind, h)][self.j * 128:(self.j + 1) * 128, :]
            if self.kind[0] == "v":
                a = a.rearrange("a (b d) -> (a b) d", d=128)
            if rest is not None:
                a = a[rest]
            return a

    def xl_view(kind):
        return _View(kind, None)

    def xa_view(kind, j):
        return _View(kind, j)

    def xchg_cc(kind, idx):
        u = (kind, idx)
        P.cc(lambda e, u=u: e.collective_compute("AllGather", ALU.bypass, replica_groups=[[0, 1, 2, 3], [4, 5, 6, 7]],
                                               ins=[xl[u].ap()], outs=[xa[u].ap()]),
             reads=[("xl", kind, idx, 0), ("xl", kind, idx, 1)], writes=[("xa", kind, idx)])

    def inproj_phase(l):
        S.reset()
        rope = S.alloc([4, T], F32)
        P.dma("sp", rope.rearrange("p a t -> p (a t)"), rope_in, writes=["rope"])
        qb16 = [S.alloc([512], BF16) for _ in range(3)]
        t1 = [S.alloc([512], F32) for _ in range(2)]
        t2 = [S.alloc([512], F32) for _ in range(2)]
        ob = [S.alloc([512], BF16) for _ in range(3)]
        vtok = [S.alloc([4, 128], BF16) for _ in range(2)]
        fst = [S.alloc([512], F32) for _ in range(3)]
        halo = S.alloc([HC], F32)
        P.op("dve", lambda e: e.memset(halo, 0.0), writes=["halo"])
        cnt = {"qb": 0, "t": 0, "ob": 0, "vt": 0, "fs": 0, "acc": 0, "rot": 0}
        pend = []

        def flush():
            while pend:
                f = pend.pop(0)
                if "tail" not in SKIP:
                    f()

        xlw = []
        for b in range(INPROJ_NB):
            wt, wid = w_get()
            wv = wt[:, 0:8192].rearrange("p (k c) -> p k c", c=256)
            for ci in range(2):
                m = 2 * b + ci
                kind, idx = seg_of(m)
                for tb in range(2):
                    acc = ps[cnt["acc"] % 4]
                    aid = ("ps", cnt["acc"] % 4)
                    cnt["acc"] += 1
                    for kc in range(KC):
                        P.op("pe", lambda e, acc=acc, wv=wv, kc=kc, ci=ci, tb=tb: e.matmul(
                            acc, lhsT=wv[:, kc, ci * 128:(ci + 1) * 128], rhs=hT[:, kc, tb * 512:(tb + 1) * 512],
                            start=(kc == 0), stop=(kc == KC - 1)), reads=wid + [("hT", tb)], writes=[aid])
                    flush()
                    if kind in ("qA", "kA", "qB", "kB"):
                        isA = kind[1] == "A"
                        Rm = RmA_bf if isA else RmB_bf
                        cosT = rope[:, 0 if isA else 2, tb * 512:(tb + 1) * 512]
                        sinT = rope[:, 1 if isA else 3, tb * 512:(tb + 1) * 512]
                        q16 = qb16[cnt["qb"] % 3]
                        q16id = ("qb16", cnt["qb"] % 3)
                        cnt["qb"] += 1
                        P.op("act", lambda e, q16=q16, acc=acc: e.activation(out=q16, in_=acc, func=AF.Copy),
                             reads=[aid], writes=[q16id])
                        T1 = t1[cnt["t"] % 2]
                        T2 = t2[cnt["t"] % 2]
                        tid = ("t12", cnt["t"] % 2)
                        cnt["t"] += 1
                        P.op("dve", lambda e, T1=T1, acc=acc, cosT=cosT: e.tensor_tensor(out=T1, in0=acc, in1=cosT, op=ALU.mult),
                             reads=[aid, "rope"], writes=[tid])
                        rot = ps[4 + cnt["rot"] % 2]
                        rotid = ("ps", 4 + cnt["rot"] % 2)
                        cnt["rot"] += 1
                        O = ob[cnt["ob"] % 3]
                        oid = ("ob", cnt["ob"] % 3)
                        cnt["ob"] += 1
                        if kind[0] == "q":
                            dst = (qsA if isA else qsB)[idx, :, tb * 512:(tb + 1) * 512]
                            did = (kind, idx, tb)
                        else:
                            dst = xl_view(kind)[idx, :, tb * 512:(tb + 1) * 512]
                            did = ("xl", kind, idx, tb)

                        def tail(Rm=Rm, q16=q16, q16id=q16id, rot=rot, rotid=rotid, T1=T1, T2=T2, tid=tid,
                                 sinT=sinT, O=O, oid=oid, dst=dst, did=did, kind=kind, idx=idx, tb=tb):
                            P.op("pe", lambda e: e.matmul(rot, lhsT=Rm, rhs=q16, start=True, stop=True),
                                 reads=[q16id, "cbf"], writes=[rotid])
                            P.op("dve", lambda e: e.tensor_tensor(out=T2, in0=rot, in1=sinT, op=ALU.mult),
                                 reads=[rotid, "rope"], writes=[(tid, 2)])
                            P.op("dve", lambda e: e.tensor_tensor(out=O, in0=T1, in1=T2, op=ALU.add),
                                 reads=[tid, (tid, 2)], writes=[oid])
                            P.dma("sp", dst, O, reads=[oid], writes=[did])
                            if kind[0] == "k" and tb == 1:
                                xchg_cc(kind, idx)
                        pend.append(tail)
                    elif kind in ("vA", "vB"):
                        q16 = qb16[cnt["qb"] % 3]
                        q16id = ("qb16", cnt["qb"] % 3)
                        cnt["qb"] += 1
                        P.op("act", lambda e, q16=q16, acc=acc: e.activation(out=q16, in_=acc, func=AF.Copy),
                             reads=[aid], writes=[q16id])
                        V = vtok[cnt["vt"] % 2]
                        vid = ("vtok", cnt["vt"] % 2)
                        cnt["vt"] += 1
                        dst = xl_view(kind)[idx].rearrange("(t p) d -> p t d", p=128)[:, 4 * tb:4 * tb + 4, :]
                        did = ("xl", kind, idx, tb)

                        def tailv(q16=q16, q16id=q16id, V=V, vid=vid, dst=dst, did=did, kind=kind, idx=idx, tb=tb):
                            for j in range(4):
                                P.op("pe", lambda e, j=j: e.transpose(psT[:, j * 128:(j + 1) * 128], q16[:, j * 128:(j + 1) * 128], ident_bf),
                                     reads=[q16id, "cbf"], writes=["psT"])
                            P.op("dve", lambda e: e.tensor_copy(out=V.rearrange("p a b -> p (a b)"), in_=psT[:, 0:512]),
                                 reads=["psT"], writes=[vid])
                            P.dma("sp", dst, V, reads=[vid], writes=[did])
                            if tb == 1:
                                xchg_cc(kind, idx)
                        pend.append(tailv)
                    else:
                        F_ = fst[cnt["fs"] % 3]
                        fid = ("fst", cnt["fs"] % 3)
                        cnt["fs"] += 1
                        P.op("act", lambda e, F_=F_, acc=acc: e.activation(out=F_, in_=acc, func=AF.Copy),
                             reads=[aid], writes=[fid])
                        P.dma("sp", fsc[kind][idx, :, tb * 512:(tb + 1) * 512], F_, reads=[fid], writes=[(kind, idx, tb)])
                        if tb == 1 and kind in ("u", "gc", "hd"):
                            if kind == "u":
                                hs = halo[:, idx * 16:(idx + 1) * 16]
                                src = F_[:, 496:512]
                            else:
                                o0 = 128 + (0 if kind == "gc" else 14) + idx * 2
                                hs = halo[:, o0:o0 + 2]
                                src = F_[:, 510:512]
                            P.op("dve", lambda e, hs=hs, src=src: e.tensor_copy(out=hs, in_=src), reads=[fid], writes=["halo"])
        flush()
        P.dma("sp", hl.ap(), halo, reads=["halo"], writes=["hl"])
        if "hl" not in SKIP:
          P.cc(lambda e: e.collective_compute("AllGather", ALU.bypass, replica_groups=[[0, 1, 2, 3], [4, 5, 6, 7]],
                                            ins=[hl.ap()], outs=[ha.ap()]), reads=["hl"], writes=["ha"])
        P.barrier()

    mixT = hT

    def cd_phase(l):
        S.reset()
        hall = S.alloc([4, HC], F32)
        hsel = S.alloc([HC], F32)
        P.dma("sp", hall, ha.ap().rearrange("(j p) c -> p j c", p=128), reads=["ha"], writes=["hall"])
        P.op("dve", lambda e: e.tensor_scalar(out=hsel, in0=hall[:, 0, :], scalar1=pcol("oh", 0), scalar2=None, op0=ALU.mult),
             reads=["hall", "prm"], writes=["hsel"])
        for j in range(1, 4):
            P.op("dve", lambda e, j=j: e.scalar_tensor_tensor(out=hsel, in0=hall[:, j, :], scalar=pcol("oh", j), in1=hsel,
                                                            op0=ALU.mult, op1=ALU.add), reads=["hall", "prm", "hsel"], writes=["hsel"])
        pwf = S.alloc([2048], F32)
        pwb = S.alloc([4, 2, 256], BF16)
        P.dma("sp", pwf, poolw_in[:, l * 2048:(l + 1) * 2048], writes=["pwf"])
        P.op("act", lambda e: e.activation(out=pwb.rearrange("p a b c -> p (a b c)"), in_=pwf, func=AF.Copy), reads=["pwf"], writes=["pwb"])
        pooled = S.alloc([8, T], BF16)
        ue = [S.alloc([16 + T], F32) for _ in range(2)]
        sa = [S.alloc([16 + T], F32) for _ in range(2)]
        sbb = [S.alloc([16 + T], F32) for _ in range(2)]
        for ch in range(8):
            g = ch // 2
            w = C_WIN[g]
            E = ue[ch % 2]
            eid = ("ue", ch % 2)
            A_, B_ = sa[ch % 2], sbb[ch % 2]
            P.dma("sp", E[:, 16:16 + T], fsc["u"][ch], reads=[("u", ch, 0), ("u", ch, 1)], writes=[eid])
            P.op("act", lambda e, E=E, ch=ch: e.activation(out=E[:, 0:16], in_=hsel[:, ch * 16:(ch + 1) * 16], func=AF.Copy),
                 reads=["hsel"], writes=[eid])
            N = 16 + T
            cur, curid, sh = E, eid, 1
            k = 0
            while sh < w:
                dst = A_ if k % 2 == 0 else B_
                dstid = ("sab", ch % 2, k % 2)
                P.op("dve", lambda e, dst=dst, cur=cur, sh=sh: e.tensor_tensor(out=dst[:, sh:N], in0=cur[:, sh:N], in1=cur[:, 0:N - sh], op=ALU.add),
                     reads=[curid, eid], writes=[dstid])
                cur, curid = dst, dstid
                sh *= 2
                k += 1
            P.op("dve", lambda e, cur=cur, E=E, ch=ch, w=w: e.scalar_tensor_tensor(
                out=pooled[:, ch, 16:T], in0=cur[:, 32:16 + T], scalar=1.0 / w, in1=E[:, 32:16 + T], op0=ALU.mult, op1=ALU.subtract),
                reads=[curid, eid], writes=[("pooled", ch)])
            io = PM["icnt"][0] + g * 16
            P.op("dve", lambda e, cur=cur, io=io: e.tensor_tensor(out=cur[:, 0:16], in0=cur[:, 16:32], in1=prm[:, io:io + 16], op=ALU.mult),
                 reads=[curid, "prm"], writes=[curid])
            P.op("dve", lambda e, cur=cur, E=E, ch=ch: e.tensor_tensor(out=pooled[:, ch, 0:16], in0=cur[:, 0:16], in1=E[:, 16:32], op=ALU.subtract),
                 reads=[curid, eid], writes=[("pooled", ch)])
        na = 0
        for g in range(4):
            for dc in range(2):
                for tb in range(2):
                    acc = ps[na % 4]
                    aid = ("ps", na % 4)
                    na += 1
                    for cc_ in range(2):
                        P.op("pe", lambda e, acc=acc, g=g, cc_=cc_, dc=dc, tb=tb: e.matmul(
                            acc, lhsT=pwb[:, g, cc_, dc * 128:(dc + 1) * 128], rhs=pooled[:, 2 * g + cc_, tb * 512:(tb + 1) * 512],
                            start=(cc_ == 0), stop=(cc_ == 1)), reads=["pwb", ("pooled", 2 * g), ("pooled", 2 * g + 1)], writes=[aid])
                    P.op("act", lambda e, acc=acc, g=g, dc=dc, tb=tb: e.activation(
                        out=mixT[:, 17 + 2 * g + dc, tb * 512:(tb + 1) * 512], in_=acc, func=AF.Copy, scale=pcol("pscale", l * 8 + 2 * g + dc)),
                        reads=[aid, "prm"], writes=[("hT", tb)])
        ce = [S.alloc([2 + T], F32) for _ in range(2)]
        he = [S.alloc([2 + T], F32) for _ in range(2)]
        gbt = [S.alloc([T], F32) for _ in range(2)]
        yy = [S.alloc([T], F32) for _ in range(2)]
        for ch in range(7):
            C_, H_, G_, Y_ = ce[ch % 2], he[ch % 2], gbt[ch % 2], yy[ch % 2]
            cid, hid, gid, yid = ("ce", ch % 2), ("he", ch % 2), ("gbt", ch % 2), ("yy", ch % 2)
            P.dma("sp", C_[:, 2:2 + T], fsc["gc"][ch], reads=[("gc", ch, 0), ("gc", ch, 1)], writes=[cid])
            P.dma("sp", H_[:, 2:2 + T], fsc["hd"][ch], reads=[("hd", ch, 0), ("hd", ch, 1)], writes=[hid])
            P.dma("sp", G_, fsc["gb"][ch], reads=[("gb", ch, 0), ("gb", ch, 1)], writes=[gid])
            P.op("act", lambda e, C_=C_, ch=ch: e.activation(out=C_[:, 0:2], in_=hsel[:, 128 + ch * 2:130 + ch * 2], func=AF.Copy),
                 reads=["hsel"], writes=[cid])
            P.op("act", lambda e, H_=H_, ch=ch: e.activation(out=H_[:, 0:2], in_=hsel[:, 142 + ch * 2:144 + ch * 2], func=AF.Copy),
                 reads=["hsel"], writes=[hid])
            P.op("dve", lambda e, C_=C_, H_=H_: e.tensor_tensor(out=C_, in0=C_, in1=H_, op=ALU.mult), reads=[cid, hid], writes=[cid])
            wo = PM["convw"][0] + l * 21
            P.op("dve", lambda e, C_=C_, Y_=Y_, wo=wo, ch=ch: e.tensor_scalar(out=Y_, in0=C_[:, 2:2 + T], scalar1=prm[:, wo + 14 + ch:wo + 15 + ch],
                                                                           scalar2=None, op0=ALU.mult), reads=[cid, "prm"], writes=[yid])
            P.op("dve", lambda e, C_=C_, Y_=Y_, wo=wo, ch=ch: e.scalar_tensor_tensor(out=Y_, in0=C_[:, 1:1 + T], scalar=prm[:, wo + 7 + ch:wo + 8 + ch],
                                                                                   in1=Y_, op0=ALU.mult, op1=ALU.add), reads=[cid, "prm", yid], writes=[yid])
            P.op("dve", lambda e, C_=C_, Y_=Y_, wo=wo, ch=ch: e.scalar_tensor_tensor(out=Y_, in0=C_[:, 0:T], scalar=prm[:, wo + ch:wo + ch + 1],
                                                                                   in1=Y_, op0=ALU.mult, op1=ALU.add), reads=[cid, "prm", yid], writes=[yid])
            P.op("dve", lambda e, Y_=Y_, G_=G_, ch=ch: e.tensor_tensor(out=mixT[:, 25 + ch, :], in0=Y_, in1=G_, op=ALU.mult),
                 reads=[yid, gid], writes=[("hT", 0), ("hT", 1)])
        P.barrier()

    def attn_phase(l):
        S.reset()
        rt = S.alloc([5, 512], F32)
        P.dma("sp", rt.rearrange("p a t -> p (a t)"), rtab_in, writes=["rt"])
        kt = [S.alloc([4, T], BF16) for _ in range(2)]
        vt = [S.alloc([32, 128], BF16) for _ in range(2)]
        qp = [S.alloc([2, T], BF16) for _ in range(2)]
        et = [S.alloc([512], BF16) for _ in range(4)]
        pt = [S.alloc([512], BF16) for _ in range(4)]
        mt = [S.alloc([512], BF16) for _ in range(2)]
        ft = [S.alloc([512], F32) for _ in range(6)]
        for i in range(2):
            P.op("dve", lambda e, i=i: e.memset(qp[i].rearrange("p a t -> p (a t)"), 0.0), writes=[("qp", i)])
        c = {"e": 0, "p": 0, "s": 0, "m": 0}
        thA0 = PM["thA"][0]
        hiB0 = PM["hiB"][0]
        c0 = 8 * l
        neglam = lamv[:, c0:c0 + 1]
        sublnw = lamv[:, c0 + 1:c0 + 2]

        def load_kv(kindk, kindv, h, slot):
            K_, V_ = kt[slot], vt[slot]
            for j in range(4):
                P.dma("sp", K_[:, j, :], xa_view(kindk, j)[h], reads=[("xa", kindk, h)], writes=[("kt", slot)])
                P.dma("sp", V_[:, 8 * j:8 * j + 8, :], xa_view(kindv, j)[h].rearrange("(t p) d -> p t d", p=128),
                      reads=[("xa", kindv, h)], writes=[("vt", slot)])
            return K_, V_

        for h in range(8):
            slot = h % 2
            K_, V_ = load_kv("kA", "vA", h, slot)
            Q_ = qp[slot]
            P.dma("sp", Q_[0:64, 0, :], qsA[h, 0:64, :], reads=[("qA", h, 0), ("qA", h, 1)], writes=[("qp", slot)])
            P.dma("sp", Q_[64:128, 1, :], qsA[h, 64:128, :], reads=[("qA", h, 0), ("qA", h, 1)], writes=[("qp", slot)])
            for qb_ in range(2):
                accs = [ps[2], ps[3], ps[4], ps[5]]
                stb = [[0, 1], [6, 7]]

                def a_st(KT, K_=K_, Q_=Q_, qb_=qb_, slot=slot):
                    r = []
                    for mp in range(2):
                        bk = stb[KT % 2][mp]
                        st, sid = ps[bk], ("ps", bk)
                        P.op("pe", lambda e, st=st, KT=KT, mp=mp: e.matmul(
                            st, lhsT=K_[:, KT // 8, (KT % 8) * 128:(KT % 8 + 1) * 128], rhs=Q_[:, mp, qb_ * 512:(qb_ + 1) * 512],
                            start=True, stop=True), reads=[("kt", slot), ("qp", slot)], writes=[sid])
                        r.append((st, sid))
                    return r

                sts = a_st(0)
                for KT in range(32):
                    nxt = a_st(KT + 1) if KT + 1 < 32 else None
                    pts = []
                    for mp in range(2):
                        st, sid = sts[mp]
                        E_ = et[c["e"] % 4]
                        eid = ("et", c["e"] % 4)
                        c["e"] += 1
                        P.op("act", lambda e, E_=E_, st=st: e.activation(out=E_, in_=st, func=AF.Exp, scale=0.125), reads=[sid], writes=[eid])
                        P_ = pt[c["p"] % 4]
                        pid = ("pt", c["p"] % 4)
                        c["p"] += 1
                        P.op("dve", lambda e, P_=P_, E_=E_, KT=KT, qb_=qb_: e.scalar_tensor_tensor(
                            out=P_, in0=rt[:, 0, :], scalar=prm[:, thA0 + qb_ * 32 + KT:thA0 + qb_ * 32 + KT + 1], in1=E_,
                            op0=ALU.is_ge, op1=ALU.mult), reads=["rt", "prm", eid], writes=[pid])
                        pts.append((P_, pid))
                    for mp in range(2):
                        P_, pid = pts[mp]
                        P.op("pe", lambda e, P_=P_, V_=V_, KT=KT, mp=mp: e.matmul(accs[mp], lhsT=V_[:, KT, :], rhs=P_, start=(KT == 0), stop=(KT == 31)),
                             reads=[pid, ("vt", slot)], writes=[("ps", 2 + mp)])
                        P.op("pe", lambda e, P_=P_, KT=KT, mp=mp: e.matmul(accs[2 + mp], lhsT=ones_bf, rhs=P_, start=(KT == 0), stop=(KT == 31)),
                             reads=[pid, "cbf"], writes=[("ps", 4 + mp)])
                    sts = nxt
                r0, r1, a_, b_, d_ = ft[0], ft[1], ft[2], ft[3], ft[4]
                P.op("dve", lambda e: e.reciprocal(out=r0, in_=accs[2]), reads=[("ps", 4)], writes=[("ft", 0)])
                P.op("dve", lambda e: e.reciprocal(out=r1, in_=accs[3]), reads=[("ps", 5)], writes=[("ft", 1)])
                P.op("dve", lambda e: e.tensor_tensor(out=a_, in0=accs[0], in1=r0, op=ALU.mult), reads=[("ps", 2), ("ft", 0)], writes=[("ft", 2)])
                P.op("dve", lambda e: e.tensor_tensor(out=b_, in0=accs[1], in1=r1, op=ALU.mult), reads=[("ps", 3), ("ft", 1)], writes=[("ft", 3)])
                P.op("dve", lambda e: e.scalar_tensor_tensor(out=d_, in0=b_, scalar=neglam, in1=a_, op0=ALU.mult, op1=ALU.add),
                     reads=[("ft", 2), ("ft", 3), "lamv"], writes=[("ft", 4)])
                M_ = mt[c["m"] % 2]
                mid = ("mt", c["m"] % 2)
                c["m"] += 1
                P.op("act", lambda e, M_=M_: e.activation(out=M_, in_=d_, func=AF.Square), reads=[("ft", 4)], writes=[mid])
                P.op("pe", lambda e, M_=M_: e.matmul(ps[6], lhsT=ones_bf, rhs=M_, start=True, stop=True), reads=[mid, "cbf"], writes=[("ps", 6)])
                rs_ = ft[5]
                P.op("dve", lambda e: e.tensor_scalar(out=rs_, in0=ps[6], scalar1=1.0 / 128, scalar2=DIFF_EPS, op0=ALU.mult, op1=ALU.add),
                     reads=[("ps", 6)], writes=[("ft", 5)])
                P.op("act", lambda e: e.activation(out=rs_, in_=rs_, func=AF.Sqrt), reads=[("ft", 5)], writes=[("ft", 5)])
                P.op("dve", lambda e: e.reciprocal(out=rs_, in_=rs_), reads=[("ft", 5)], writes=[("ft", 5)])
                P.op("dve", lambda e, h=h, qb_=qb_: e.scalar_tensor_tensor(out=mixT[:, h, qb_ * 512:(qb_ + 1) * 512], in0=d_, scalar=sublnw, in1=rs_,
                                                                        op0=ALU.mult, op1=ALU.mult),
                     reads=[("ft", 4), ("ft", 5), "lamv"], writes=[("hT", qb_)])
        ut = [S.alloc([2, 512], F32) for _ in range(3)]
        stt = [S.alloc([2, 512], F32) for _ in range(3)]
        nh = 0
        for hh in range(3):
            for g in range(3):
                head = g * 3 + hh
                slot = nh % 2
                nh += 1
                K_, V_ = load_kv("kB", "vB", head, slot)
                Q_ = qp[slot]
                P.dma("sp", Q_[:, 0, :], qsB[head], reads=[("qB", head, 0), ("qB", head, 1)], writes=[("qp", slot)])
                ti = 0 if g == 0 else (1 + 2 * (g - 1))
                tu = 0 if g == 0 else (2 + 2 * (g - 1))
                for qb_ in range(2):
                    bbk = [0, 1, 6, 7]

                    def b_st(KT, K_=K_, Q_=Q_, qb_=qb_, slot=slot):
                        bk = bbk[KT % 4]
                        st, sid = ps[bk], ("ps", bk)
                        P.op("pe", lambda e, st=st, KT=KT: e.matmul(
                            st, lhsT=K_[:, KT // 8, (KT % 8) * 128:(KT % 8 + 1) * 128], rhs=Q_[:, 0, qb_ * 512:(qb_ + 1) * 512],
                            start=True, stop=True), reads=[("kt", slot), ("qp", slot)], writes=[sid])
                        return st, sid

                    LA = 3
                    stq = [b_st(k) for k in range(LA)]
                    for KT in range(32):
                        if KT + LA < 32:
                            stq.append(b_st(KT + LA))
                        st, sid = stq.pop(0)
                        E_ = et[c["e"] % 4]
                        eid = ("et", c["e"] % 4)
                        c["e"] += 1
                        P.op("act", lambda e, E_=E_, st=st: e.activation(out=E_, in_=st, func=AF.Exp, scale=128 ** -0.5), reads=[sid], writes=[eid])
                        P_ = pt[c["p"] % 4]
                        pid = ("pt", c["p"] % 4)
                        c["p"] += 1
                        ci_ = qb_ * 32 + KT
                        P.op("dve", lambda e, E_=E_, ti=ti, ci_=ci_: e.scalar_tensor_tensor(
                            out=E_, in0=rt[:, ti, :], scalar=prm[:, thA0 + ci_:thA0 + ci_ + 1], in1=E_, op0=ALU.is_ge, op1=ALU.mult),
                            reads=["rt", "prm", eid], writes=[eid])
                        P.op("dve", lambda e, P_=P_, E_=E_, tu=tu, ci_=ci_, g=g: e.scalar_tensor_tensor(
                            out=P_, in0=rt[:, tu, :], scalar=prm[:, hiB0 + g * 64 + ci_:hiB0 + g * 64 + ci_ + 1], in1=E_, op0=ALU.is_le, op1=ALU.mult),
                            reads=["rt", "prm", eid], writes=[pid])
                        P.op("pe", lambda e, P_=P_, V_=V_, KT=KT: e.matmul(ps[2], lhsT=V_[:, KT, :], rhs=P_, start=(KT == 0), stop=(KT == 31)),
                             reads=[pid, ("vt", slot)], writes=[("ps", 2)])
                        P.op("pe", lambda e, P_=P_, KT=KT: e.matmul(ps[3], lhsT=ones_bf, rhs=P_, start=(KT == 0), stop=(KT == 31)),
                             reads=[pid, "cbf"], writes=[("ps", 3)])
                    P.op("act", lambda e, g=g, qb_=qb_: e.activation(out=ut[g][:, qb_, :], in_=ps[2], func=AF.Copy), reads=[("ps", 2)], writes=[("ut", g)])
                    P.op("dve", lambda e, g=g, qb_=qb_: e.tensor_copy(out=stt[g][:, qb_, :], in_=ps[3]), reads=[("ps", 3)], writes=[("stt", g)])
            tot = stt[0]
            P.op("dve", lambda e: e.tensor_tensor(out=tot, in0=stt[0], in1=stt[1], op=ALU.add), reads=[("stt", 0), ("stt", 1)], writes=[("stt", 0)])
            P.op("dve", lambda e: e.tensor_tensor(out=tot, in0=tot, in1=stt[2], op=ALU.add), reads=[("stt", 0), ("stt", 2)], writes=[("stt", 0)])
            P.op("dve", lambda e: e.reciprocal(out=tot, in_=tot), reads=[("stt", 0)], writes=[("stt", 0)])
            for g in range(3):
                P.op("dve", lambda e, g=g, hh=hh: e.tensor_tensor(out=mixT[:, 8 + 3 * g + hh, :].rearrange("p (a b) -> p a b", a=2), in0=ut[g], in1=tot, op=ALU.mult),
                     reads=[("ut", g), ("stt", 0)], writes=[("hT", 0), ("hT", 1)])
        P.barrier()

    def resid_evac(acc, aid, m, tb, xo, xoid, cnts):
        P.op("dve", lambda e: e.tensor_tensor(out=xo, in0=acc, in1=xo, op=ALU.add), reads=[aid, xoid], writes=[xoid])
        P.dma("sp", xres[m, :, tb * 512:(tb + 1) * 512], xo, reads=[xoid], writes=[("x", m, tb)])

    def outproj_phase(l):
        S.reset()
        xo = [S.alloc([512], F32) for _ in range(4)]
        n = 0
        for b in range(16):
            wt, wid = w_get()
            wv = wt[:, 0:8192].rearrange("p (k c) -> p k c", c=256)
            for ci in range(2):
                m = 2 * b + ci
                for tb in range(2):
                    X_ = xo[n % 4]
                    xid = ("xo", n % 4)
                    acc = ps[n % 4]
                    aid = ("ps", n % 4)
                    n += 1
                    P.dma("sp", X_, xres[m, :, tb * 512:(tb + 1) * 512], reads=[("x", m, tb)], writes=[xid])
                    for kc in range(KC):
                        P.op("pe", lambda e, acc=acc, wv=wv, kc=kc, ci=ci, tb=tb: e.matmul(
                            acc, lhsT=wv[:, kc, ci * 128:(ci + 1) * 128], rhs=mixT[:, kc, tb * 512:(tb + 1) * 512],
                            start=(kc == 0), stop=(kc == KC - 1)), reads=wid + [("hT", tb)], writes=[aid])
                    resid_evac(acc, aid, m, tb, X_, xid, None)
        P.barrier()

    def mlp_phase(l):
        S.reset()
        actg = S.alloc([KCG, T], BF16)
        rl = [S.alloc([512], F32) for _ in range(2)]
        xo = [S.alloc([512], F32) for _ in range(4)]
        n = 0
        nr = 0
        for g in range(4):
            for b in range(NBU // 4):
                wt, wid = w_get()
                wv = wt[:, 0:8192].rearrange("p (k c) -> p k c", c=256)
                for ci in range(2):
                    fc = 2 * b + ci
                    for tb in range(2):
                        acc = ps[n % 4]
                        aid = ("ps", n % 4)
                        n += 1
                        for kc in range(KC):
                            P.op("pe", lambda e, acc=acc, wv=wv, kc=kc, ci=ci, tb=tb: e.matmul(
                                acc, lhsT=wv[:, kc, ci * 128:(ci + 1) * 128], rhs=hT[:, kc, tb * 512:(tb + 1) * 512],
                                start=(kc == 0), stop=(kc == KC - 1)), reads=wid + [("hT", tb)], writes=[aid])
                        R_ = rl[nr % 2]
                        rid = ("rl", nr % 2)
                        nr += 1
                        P.op("act", lambda e, R_=R_, acc=acc: e.activation(out=R_, in_=acc, func=AF.Relu), reads=[aid], writes=[rid])
                        P.op("dve", lambda e, R_=R_, fc=fc, tb=tb: e.tensor_tensor(out=actg[:, fc, tb * 512:(tb + 1) * 512], in0=R_, in1=R_, op=ALU.mult),
                             reads=[rid], writes=[("actg", tb)])
            for cb in range(16):
                wt, wid = w_get()
                wv = wt[:, 0:KCG * 256].rearrange("p (k c) -> p k c", c=256)
                for ci in range(2):
                    m = 2 * cb + ci
                    for tb in range(2):
                        X_ = xo[n % 4]
                        xid = ("xo", n % 4)
                        acc = ps[n % 4]
                        aid = ("ps", n % 4)
                        n += 1
                        P.dma("sp", X_, xres[m, :, tb * 512:(tb + 1) * 512], reads=[("x", m, tb)], writes=[xid])
                        for fc in range(KCG):
                            P.op("pe", lambda e, acc=acc, wv=wv, fc=fc, ci=ci, tb=tb: e.matmul(
                                acc, lhsT=wv[:, fc, ci * 128:(ci + 1) * 128], rhs=actg[:, fc, tb * 512:(tb + 1) * 512],
                                start=(fc == 0), stop=(fc == KCG - 1)), reads=wid + [("actg", tb)], writes=[aid])
                        resid_evac(acc, aid, m, tb, X_, xid, None)
        P.barrier()

    def dump(name, src_sb):
        if debug and name in dbg:
            P.barrier()
            P.dma("sp", dbg[name].rearrange("k p t -> p k t"), src_sb, reads=[], writes=[("dbg", name)])
            P.barrier()

    def program():
        st = 0

        def chk():
            nonlocal st
            st += 1
            return stop is not None and st > stop
        if chk():
            return
        for l in range(L):
            norm_phase("g_mix", l)
            if l == 0:
                dump("h", hT[:])
            if chk():
                return
            inproj_phase(l)
            if l == 0 and "bg" not in SKIP:
                def mlp_gather(l_):
                    if NBU // 4 >= 8:
                        q = NBU // 32
                        for g in range(4):
                            cast_and_gather("up", l_, range(g * q, (g + 1) * q))
                            cast_and_gather("dn", l_, range(2 * g, 2 * g + 2))
                    else:
                        cast_and_gather("up", l_)
                        cast_and_gather("dn", l_)
                mlp_gather(0)
                for l2 in range(1, L):
                    cast_and_gather("in", l2)
                    cast_and_gather("out", l2)
                    mlp_gather(l2)
            if l == 0 and debug:
                P.barrier()
                P.dma("sp", dbg["qA"], qsA.ap(), reads=[], writes=[("dbg", "qA")])
                P.barrier()
            if chk():
                return
            cd_phase(l)
            if chk():
                return
            attn_phase(l)
            if l == 0:
                dump("mix", mixT[:])
            if chk():
                return
            outproj_phase(l)
            if l == 0 and debug:
                P.barrier()
                P.dma("sp", dbg["x1"], xres.ap(), reads=[], writes=[("dbg", "x1")])
                P.barrier()
            if chk():
                return
            norm_phase("g_mlp", l)
            if chk():
                return
            mlp_phase(l)
            if chk():
                return
        norm_phase("g_fin", 0, final=True)
    program()
    P.barrier()
    P.op("sp", lambda e: e.nop())
    P.emit(nc, es)
    es.close()
    return nc


def _rope_tab(pos, dim):
    inv = np.power(np.float32(10000.0), -(np.arange(0, dim, 2, dtype=np.float32) / np.float32(dim))).astype(np.float32)
    ang = (pos.astype(np.float32)[:, None] * inv[None, :]).astype(np.float32)
    return np.cos(ang.astype(np.float64)).astype(np.float32), np.sin(ang.astype(np.float64)).astype(np.float32)


def _tile_cols(w, nblk_local, r, kcn, blocks=None):
    K, N = w.shape
    if blocks is None:
        c0 = r * nblk_local * 256
        sub = w[:, c0:c0 + nblk_local * 256]
    else:
        sub = np.concatenate([w[:, gb * 256:(gb + 1) * 256] for gb in blocks], axis=1)
    t = sub.reshape(kcn, 128, nblk_local, 256).transpose(2, 1, 0, 3)
    return np.ascontiguousarray(t).reshape(nblk_local * 128, kcn * 256)


def prep_inputs(cfg, inp):
    L, KCG, NBU = cfg.L, cfg.KCG, cfg.NBU
    PM, NPRM = _prm_map(L)
    x = np.asarray(inp["x"], np.float32)
    w_in = np.asarray(inp["w_in"], np.float32)
    w_out = np.asarray(inp["w_out"], np.float32)
    w_up = np.asarray(inp["w_up"], np.float32)
    w_dn = np.asarray(inp["w_down"], np.float32)
    maps = []
    kl = np.arange(128)[:, None]
    ql = np.arange(512)[None, :]
    R1 = (ql - kl).astype(np.float32)
    rt = [R1]
    for d in (4, 16):
        ok = ((ql - kl) % d) == 0
        rt.append(np.where(ok, R1, -1e9).astype(np.float32))
        rt.append(np.where(ok, R1, 1e9).astype(np.float32))
    rtab = np.concatenate(rt, axis=1)
    RmA = np.zeros((128, 128), np.float32)
    RmB = np.zeros((128, 128), np.float32)
    for m in range(128):
        if (m % 64) < 32:
            RmA[m + 32, m] = -1.0
        else:
            RmA[m - 32, m] = 1.0
        if m < 64:
            RmB[m + 64, m] = -1.0
        else:
            RmB[m - 64, m] = 1.0
    ident = np.eye(128, dtype=np.float32)
    poolw = np.asarray(inp["pool_w"], np.float32)
    pw = poolw.reshape(L, 4, 2, 128, 256).transpose(3, 0, 1, 2, 4).reshape(128, L * 2048)
    pw = np.ascontiguousarray(pw)

    def put(prm, name, arr):
        o, n = PM[name]
        arr = np.asarray(arr, np.float32)
        if arr.ndim == 1:
            assert arr.shape[0] == n, (name, arr.shape, n)
            prm[:, o:o + n] = arr[None, :]
        else:
            a2 = arr.reshape(128, -1)
            assert a2.shape[1] == n, (name, a2.shape, n)
            prm[:, o:o + n] = a2

    for c in range(NCORE):
        bch, r = c // 4, c % 4
        m = {}
        m["xT"] = np.ascontiguousarray(x[bch, r * T:(r + 1) * T, :].T)
        m["w_in_s"] = np.concatenate([_tile_cols(w_in[l], 5, c, 32) for l in range(L)], axis=0)
        m["w_out_s"] = np.concatenate([_tile_cols(w_out[l], 2, c, 32) for l in range(L)], axis=0)
        ublk = [up_local_to_global(NBU, c, bl) for bl in range(NBU // 8)]
        m["w_up_s"] = np.concatenate([_tile_cols(w_up[l], NBU // 8, c, 32, ublk) for l in range(L)], axis=0)
        dn = []
        for l in range(L):
            for g in range(4):
                for cbl in range(2):
                    cb = 2 * c + cbl
                    sub = w_dn[l][g * cfg.FG:(g + 1) * cfg.FG, cb * 256:(cb + 1) * 256]
                    dn.append(np.ascontiguousarray(sub.reshape(KCG, 128, 256).transpose(1, 0, 2)).reshape(128, KCG * 256))
        m["w_dn_s"] = np.concatenate(dn, axis=0)
        prm = np.zeros((128, NPRM), np.float32)
        put(prm, "g_mix", np.asarray(inp["norm_mix"], np.float32).reshape(L, 32, 128).transpose(2, 0, 1))
        put(prm, "g_mlp", np.asarray(inp["norm_mlp"], np.float32).reshape(L, 32, 128).transpose(2, 0, 1))
        put(prm, "g_fin", np.asarray(inp["norm_final"], np.float32).reshape(32, 128).T)
        put(prm, "subln", np.asarray(inp["diff_subln"], np.float32).T)
        put(prm, "pscale", np.asarray(inp["pool_scale"], np.float32).reshape(L, 8, 128).transpose(2, 0, 1))
        put(prm, "convw", np.asarray(inp["conv_w"], np.float32).reshape(L, 3, 7, 128).transpose(3, 0, 1, 2))
        put(prm, "lam", np.asarray(inp["diff_lambda"], np.float32).reshape(-1))
        th = np.zeros(64, np.float32)
        hi = np.zeros((3, 64), np.float32)
        for qb_ in range(2):
            for KT in range(32):
                dl = 128 * KT - (1024 * r + 512 * qb_)
                th[qb_ * 32 + KT] = dl
                for g in range(3):
                    hi[g, qb_ * 32 + KT] = dl + 128 * B_DIL[g]
        put(prm, "thA", th)
        put(prm, "hiB", hi.reshape(-1))
        oh = np.zeros(4, np.float32)
        if r >= 1:
            oh[r - 1] = 1.0
        put(prm, "oh", oh)
        ic = np.zeros((4, 16), np.float32)
        for g in range(4):
            for t in range(16):
                ic[g, t] = 1.0 / (min(t + 1, C_WIN[g]) if r == 0 else C_WIN[g])
        put(prm, "icnt", ic.reshape(-1))
        put(prm, "RmA", RmA)
        put(prm, "RmB", RmB)
        put(prm, "ident", ident)
        m["prm"] = prm
        pos = np.arange(r * T, (r + 1) * T)
        ca, sa_ = _rope_tab(pos, 64)
        cb_, sb_ = _rope_tab(pos, 128)
        pA = np.arange(128) % 32
        pB = np.arange(128) % 64
        m["rope"] = np.ascontiguousarray(np.concatenate([ca.T[pA], sa_.T[pA], cb_.T[pB], sb_.T[pB]], axis=1))
        m["rtab"] = rtab
        m["poolw"] = pw
        maps.append(m)
    return maps


_NC_CACHE = {}


def run(cfg, inputs, debug=False, stop=None):
    key = (cfg.L, cfg.DFF, debug, stop)
    if key not in _NC_CACHE:
        _NC_CACHE[key] = build(cfg, debug, stop)
    nc = _NC_CACHE[key]
    maps = prep_inputs(cfg, inputs)
    res = run_bass_kernel_spmd(nc, maps, core_ids=list(range(NCORE)))
    x = np.asarray(inputs["x"])
    out = np.empty(x.shape, np.float32)
    for c in range(NCORE):
        out[c // 4, (c % 4) * T:(c % 4 + 1) * T, :] = res.results[c]["outT"].T
    return out, res


def kernel(**inputs):
    cfg = Cfg(depth=2, d_ff=16384)
    out, _ = run(cfg, inputs)
    return out
```

```python
import math
import numpy as np
from contextlib import ExitStack
import concourse.bass as bass
import concourse.mybir as mybir
from concourse.bass_utils import run_bass_kernel_spmd

F32 = mybir.dt.float32
BF16 = mybir.dt.bfloat16
ALU = mybir.AluOpType
AF = mybir.ActivationFunctionType
AX = mybir.AxisListType

D = 4096
T = 1024
KC = 32
NCORE = 8
INW = 10240
NORM_EPS = 1e-6
DIFF_EPS = 1e-5
B_DIL = (1, 4, 16)
C_WIN = (2, 4, 8, 16)
XL_ROWS = 4352
INPROJ_NB = 40
SKIP = set()
HC = 160


class Cfg:
    def __init__(self, depth=2, d_ff=16384):
        self.L = depth
        self.DFF = d_ff
        self.FG = d_ff // 4
        self.KCG = self.FG // 128
        self.NBU = d_ff // 256
        assert self.NBU % 8 == 0


class _Op:
    __slots__ = ("eng", "fn", "deps", "kind", "sig", "cnt", "dsem", "dval", "prev")


class Prog:
    ENGS = ("pe", "act", "dve", "pool", "sp")
    NDS = 8

    def __init__(self):
        self.ops = []
        self.lastw = {}
        self.rd_c = {}
        self.rd_d = {}
        self.bar = []
        self.last_on = {e: None for e in self.ENGS}
        self.dma_hist = {e: [] for e in self.ENGS}
        self.last_cc = None
        self.last_cc_fg = None
        self.dma_fg = {e: [] for e in self.ENGS}
        self.ncc = 0

    @staticmethod
    def _is_psum(b):
        return b in ("pn", "psT") or (isinstance(b, tuple) and len(b) == 2 and b[0] == "ps")

    def _add(self, eng, fn, reads, writes, kind, bg=False):
        pr = [b for b in reads if self._is_psum(b)]
        if pr:
            writes = list(writes) + pr
            reads = [b for b in reads if not self._is_psum(b)]
        deps = set(self.bar)
        for b in reads:
            w = self.lastw.get(b)
            if w is not None:
                deps.add(w)
        for b in writes:
            w = self.lastw.get(b)
            if w is not None:
                deps.add(w)
            r = self.rd_c.get(b)
            if r:
                deps.update(r.values())
            r = self.rd_d.get(b)
            if r:
                deps.update(r)
        if kind == "cc" and self.last_cc is not None:
            deps.add(self.last_cc)
        i = len(self.ops)
        o = _Op()
        o.eng, o.fn, o.deps, o.kind, o.sig, o.cnt = eng, fn, deps, kind, False, 0
        o.dsem = o.dval = o.prev = None
        self.ops.append(o)
        for b in reads:
            if kind == "c":
                self.rd_c.setdefault(b, {})[eng] = i
            else:
                self.rd_d.setdefault(b, []).append(i)
        for b in writes:
            self.lastw[b] = i
            self.rd_c[b] = {}
            self.rd_d[b] = []
        if not bg:
            self.last_on[eng] = i
        if kind == "dma":
            k = len(self.dma_hist[eng])
            o.dsem = (eng, k % self.NDS)
            o.dval = 16 * (k // self.NDS + 1)
            if k >= self.NDS:
                o.prev = (o.dsem, 16 * (k // self.NDS))
            self.dma_hist[eng].append(i)
            if not bg:
                self.dma_fg[eng].append(i)
        if kind == "cc":
            self.ncc += 1
            o.dsem = ("cc", 0)
            o.dval = self.ncc
            self.last_cc = i
            if not bg:
                self.last_cc_fg = i
        return i

    def op(self, eng, fn, reads=(), writes=()):
        return self._add(eng, fn, reads, writes, "c")

    def dma(self, eng, out, in_, reads=(), writes=(), bg=False, **kw):
        return self._add(eng, lambda e: e.dma_start(out=out, in_=in_, **kw), reads, writes, "dma", bg)

    def cc(self, fn, reads=(), writes=(), bg=False):
        return self._add("pool", fn, reads, writes, "cc", bg)

    def barrier(self):
        b = [v for v in self.last_on.values() if v is not None]
        for e in self.ENGS:
            b += self.dma_fg[e][-self.NDS:]
        if self.last_cc_fg is not None:
            b.append(self.last_cc_fg)
        self.bar = b

    def emit(self, nc, es):
        ops = self.ops
        for x in ops:
            for d in x.deps:
                y = ops[d]
                if y.kind == "c" and not (x.kind == "c" and x.eng == y.eng == "pe"):
                    y.sig = True
        cnt = {e: 0 for e in self.ENGS}
        for o in ops:
            if o.kind == "c" and o.sig:
                cnt[o.eng] += 1
                o.cnt = cnt[o.eng]
        sem = {}
        for e in self.ENGS:
            sem[("c", e)] = es.enter_context(nc.semaphore("c_" + e))
            if self.dma_hist[e]:
                for k in range(self.NDS):
                    sem[(e, k)] = es.enter_context(nc.semaphore("d_%s%d" % (e, k)))
        sem[("cc", 0)] = es.enter_context(nc.semaphore("ccs"))
        per = {e: [] for e in self.ENGS}
        for o in ops:
            per[o.eng].append(o)
        block = es.enter_context(nc.Block())

        def run(engname, eng):
            known = {}
            for o in per[engname]:
                waits = {}
                for d in o.deps:
                    y = ops[d]
                    if y.kind == "c":
                        if o.kind == "c" and y.eng == engname == "pe":
                            continue
                        key, val = ("c", y.eng), y.cnt
                    else:
                        key, val = y.dsem, y.dval
                    if waits.get(key, 0) < val:
                        waits[key] = val
                if o.prev is not None:
                    key, val = o.prev
                    if waits.get(key, 0) < val:
                        waits[key] = val
                for key, val in waits.items():
                    if known.get(key, 0) < val:
                        eng.wait_ge(sem[key], val)
                        known[key] = val
                ins = o.fn(eng)
                if o.kind == "c":
                    if o.sig:
                        ins.then_inc(sem[("c", engname)], 1)
                elif o.kind == "dma":
                    ins.then_inc(sem[o.dsem], 16)
                else:
                    ins.then_inc(sem[o.dsem], 1)

        @block.tensor
        def _(e):
            run("pe", e)

        @block.scalar
        def _(e):
            run("act", e)

        @block.vector
        def _(e):
            run("dve", e)

        @block.gpsimd
        def _(e):
            run("pool", e)

        @block.sync
        def _(e):
            run("sp", e)


class Region:
    def __init__(self, ap_f32, nbytes):
        self.ap = ap_f32
        self.n = nbytes
        self.off = 0

    def reset(self):
        self.off = 0

    def alloc(self, shape, dt):
        es = 4 if dt == F32 else 2
        n = 1
        for s in shape:
            n *= s
        nb = (n * es + 31) // 32 * 32
        assert self.off + nb <= self.n, ("scratch overflow", self.off, nb, self.n)
        v = self.ap[:, self.off // 4:(self.off + nb) // 4]
        if dt != F32:
            v = v.bitcast(dt)
        v = v[:, 0:n]
        self.off += nb
        if len(shape) == 2:
            v = v.rearrange("p (a b) -> p a b", a=shape[0])
        elif len(shape) == 3:
            v = v.rearrange("p (a b c) -> p a b c", a=shape[0], b=shape[1])
        return v


def _prm_map(L):
    m = {}
    o = 0

    def put(name, n):
        nonlocal o
        m[name] = (o, n)
        o += n
    put("g_mix", L * 32)
    put("g_mlp", L * 32)
    put("g_fin", 32)
    put("subln", L)
    put("pscale", L * 8)
    put("convw", L * 21)
    put("lam", L * 256)
    put("thA", 64)
    put("hiB", 192)
    put("oh", 4)
    put("icnt", 64)
    put("RmA", 128)
    put("RmB", 128)
    put("ident", 128)
    return m, o


def up_local_to_global(NBU, r, bl):
    nbu = NBU // 8
    if NBU // 4 >= 8:
        q = NBU // 32
        g, j = bl // q, bl % q
        return g * (NBU // 4) + r * q + j
    return r * nbu + bl


def up_global_to_local(NBU, gb):
    nbu = NBU // 8
    if NBU // 4 >= 8:
        q = NBU // 32
        g, rem = gb // (NBU // 4), gb % (NBU // 4)
        return rem // q, g * q + rem % q
    return gb // nbu, gb % nbu


def seg_of(m):
    if m < 8:
        return ("qA", m)
    if m < 16:
        return ("kA", m - 8)
    if m < 24:
        return ("vA", m - 16)
    if m < 33:
        return ("qB", m - 24)
    if m < 42:
        return ("kB", m - 33)
    if m < 51:
        return ("vB", m - 42)
    if m < 59:
        return ("u", m - 51)
    if m < 66:
        return ("gb", m - 59)
    if m < 73:
        return ("gc", m - 66)
    return ("hd", m - 73)


def build(cfg, debug=False, stop=None):
    L, KCG, NBU = cfg.L, cfg.KCG, cfg.NBU
    nc = bass.Bass("TRN2", target_bir_lowering=False)
    P = Prog()
    es = ExitStack()
    PM, NPRM = _prm_map(L)

    def ext_in(name, shape, dt=F32):
        return nc.dram_tensor(name, shape, dt, kind="ExternalInput").ap()
    xT_in = ext_in("xT", [D, T])
    w_in_s = ext_in("w_in_s", [L * 5 * 128, 8192])
    w_out_s = ext_in("w_out_s", [L * 2 * 128, 8192])
    w_up_s = ext_in("w_up_s", [L * (NBU // 8) * 128, 8192])
    w_dn_s = ext_in("w_dn_s", [L * 8 * 128, KCG * 256])
    prm_in = ext_in("prm", [128, NPRM])
    rope_in = ext_in("rope", [128, 4 * T])
    rtab_in = ext_in("rtab", [128, 5 * 512])
    poolw_in = ext_in("poolw", [128, L * 2048])
    outT = nc.dram_tensor("outT", [D, T], F32, kind="ExternalOutput").ap()

    def internal(name, shape, dt):
        return nc.dram_tensor(name, shape, dt)
    wshapes = {"in": (5, 8192), "out": (2, 8192), "up": (NBU // 8, 8192), "dn": (8, KCG * 256)}
    wsrc = {"in": w_in_s, "out": w_out_s, "up": w_up_s, "dn": w_dn_s}
    wl, w4, wa = {}, {}, {}
    for l in range(L):
        for nm, (nb, cols) in wshapes.items():
            qc = cols // 4
            for bl in range(nb):
                wl[(nm, l, bl)] = internal("wl_%s%d_%d" % (nm, l, bl), [4, 128, qc], BF16)
                for q in range(4):
                    w4[(nm, l, bl, q)] = internal("w4_%s%d_%d_%d" % (nm, l, bl, q), [4 * 128, qc], BF16)
                    wa[(nm, l, bl, q)] = internal("wa_%s%d_%d_%d" % (nm, l, bl, q), [8 * 128, qc], BF16)

    def w_tile(nm, l, r, bl):
        nb, cols = wshapes[nm]
        qc = cols // 4
        pieces = [(wa[(nm, l, bl, q)][r * 128:(r + 1) * 128, :], q * qc, qc, ("wa", nm, l, bl, q)) for q in range(4)]
        return pieces, cols

    xres = internal("xres", [KC, 128, T], F32)
    qsA = internal("qsA", [8, 128, T], BF16)
    qsB = internal("qsB", [9, 128, T], BF16)
    fsc = {k: internal("fs_" + k, [n, 128, T], F32) for k, n in (("u", 8), ("gb", 7), ("gc", 7), ("hd", 7))}
    XU = [("kA", h) for h in range(8)] + [("kB", h) for h in range(9)] + [("vA", h) for h in range(8)] + [("vB", h) for h in range(9)]
    xl = {u: internal("xl_%s%d" % u, [128, 1024], BF16) for u in XU}
    xa = {u: internal("xa_%s%d" % u, [4 * 128, 1024], BF16) for u in XU}
    hl = internal("hl", [128, HC], F32)
    ha = internal("ha", [4 * 128, HC], F32)
    dbg = {}
    if debug:
        dbg["mix"] = nc.dram_tensor("dbg_mix", [KC, 128, T], BF16, kind="ExternalOutput").ap()
        dbg["h"] = nc.dram_tensor("dbg_h", [KC, 128, T], BF16, kind="ExternalOutput").ap()
        dbg["x1"] = nc.dram_tensor("dbg_x1", [KC, 128, T], F32, kind="ExternalOutput").ap()
        dbg["qA"] = nc.dram_tensor("dbg_qA", [8, 128, T], BF16, kind="ExternalOutput").ap()

    def sb(name, shape, dt):
        return es.enter_context(nc.sbuf_tensor(name, shape, dt))
    hT = sb("hT", [128, KC, T], BF16)
    NWB = 2
    wbuf = [sb("wb%d" % i, [128, 8192], BF16) for i in range(NWB)]
    prm = sb("prm_sb", [128, NPRM], F32)
    cbf = sb("cbf", [128, 4 * 128], BF16)
    lamv = sb("lamv", [128, 8 * L], F32)
    SCR_BYTES = 100 * 1024
    scr_t = sb("scr", [128, SCR_BYTES // 4], F32)
    S = Region(scr_t[:], SCR_BYTES)
    ps = [es.enter_context(nc.psum_tensor("ps%d" % i, [128, 512], F32))[:, :] for i in range(8)]
    psT = ps[7].bitcast(BF16)

    ones_bf = cbf[:, 0:128]
    ident_bf = cbf[:, 128:256]
    RmA_bf = cbf[:, 256:384]
    RmB_bf = cbf[:, 384:512]

    def pcol(name, idx=0, n=1):
        o, _ = PM[name]
        return prm[:, o + idx:o + idx + n]

    class WS:
        seq = []
        nload = 0
        nuse = 0

    def w_plan(nm, l):
        nb, cols = wshapes[nm]
        for bl in range(nb):
            for r in range(8):
                WS.seq.append(w_tile(nm, l, r, bl))

    def gb_of(nm, idx):
        nb = wshapes[nm][0]
        return (idx % 8) * nb + idx // 8

    def w_prefetch():
        while WS.nload < len(WS.seq) and WS.nload < WS.nuse + NWB:
            pieces, cols = WS.seq[WS.nload]
            j = WS.nload % NWB
            for src, c0, qc, bid in pieces:
                P.dma("sp", wbuf[j][:, c0:c0 + qc], src, reads=[bid], writes=[("wb", j, c0)])
            WS.nload += 1

    def w_get():
        w_prefetch()
        j = WS.nuse % NWB
        pieces, cols = WS.seq[WS.nuse]
        WS.nuse += 1
        return wbuf[j], [("wb", j, c0) for _, c0, _, _ in pieces]

    for l in range(L):
        w_plan("in", l)
        w_plan("out", l)
        for g in range(4):
            nbu = NBU // 8
            for gb in range(g * (NBU // 4), (g + 1) * (NBU // 4)):
                r_, bl_ = up_global_to_local(NBU, gb)
                WS.seq.append(w_tile("up", l, r_, bl_))
            for cb in range(16):
                WS.seq.append(w_tile("dn", l, cb // 2, g * 2 + cb % 2))

    G4 = [[0, 1, 2, 3], [4, 5, 6, 7]]
    G2 = [[0, 4], [1, 5], [2, 6], [3, 7]]

    def cast_and_gather(nm, l, bls=None):
        nb, cols = wshapes[nm]
        qc = cols // 4
        for bl in (range(nb) if bls is None else bls):
            src = wsrc[nm][(l * nb + bl) * 128:(l * nb + bl + 1) * 128, :]
            P.dma("pool", wl[(nm, l, bl)].ap().rearrange("q p c -> p q c"), src.rearrange("p (q c) -> p q c", q=4),
                  reads=[], writes=[("wl", nm, l, bl)], bg=True)
            for q in range(4):
                k = (nm, l, bl, q)
                P.cc(lambda e, k=k, bl=bl, q=q: e.collective_compute("AllGather", ALU.bypass, replica_groups=G4,
                                                                   ins=[wl[k[:3]][q]], outs=[w4[k].ap()]),
                     reads=[("wl", nm, l, bl)], writes=[("w4",) + k], bg=True)
                P.cc(lambda e, k=k: e.collective_compute("AllGather", ALU.bypass, replica_groups=G2,
                                                       ins=[w4[k].ap()], outs=[wa[k].ap()]),
                     reads=[("w4",) + k], writes=[("wa",) + k], bg=True)

    P.dma("sp", prm[:], prm_in, writes=["prm"])
    for kc4 in range(0, KC, 8):
        P.dma("sp", xres[kc4:kc4 + 8].rearrange("k p t -> (k p) t"), xT_in[kc4 * 128:(kc4 + 8) * 128, :],
              writes=[("x", m, tb) for m in range(kc4, kc4 + 8) for tb in range(2)])
    cast_and_gather("in", 0)
    cast_and_gather("out", 0)
    P.op("dve", lambda e: e.memset(cbf[:, 0:128], 1.0), writes=["cbf"])
    P.op("act", lambda e: e.activation(out=cbf[:, 128:256], in_=prm[:, PM["ident"][0]:PM["ident"][0] + 128], func=AF.Copy),
         reads=["prm"], writes=["cbf"])
    P.op("act", lambda e: e.activation(out=cbf[:, 256:512], in_=prm[:, PM["RmA"][0]:PM["RmA"][0] + 256], func=AF.Copy),
         reads=["prm"], writes=["cbf"])
    for l in range(L):
        lam_init = 0.8 - 0.6 * math.exp(-0.3 * l)
        lo = PM["lam"][0] + l * 256
        c0 = 8 * l
        t12 = lamv[:, c0 + 4:c0 + 6]
        S.reset()
        pr = S.alloc([128], F32)
        for i in range(2):
            P.op("dve", lambda e, i=i, lo=lo: e.tensor_tensor(out=pr[:, 0:64], in0=prm[:, lo + 128 * i:lo + 128 * i + 64],
                                                            in1=prm[:, lo + 128 * i + 64:lo + 128 * i + 128], op=ALU.mult),
                 reads=["prm"], writes=["pr"])
            P.op("dve", lambda e, i=i, c0=c0: e.reduce_sum(out=lamv[:, c0 + 4 + i:c0 + 5 + i], in_=pr[:, 0:64], axis=AX.X),
                 reads=["pr"], writes=["lamv"])
        P.op("act", lambda e, t12=t12: e.activation(out=t12, in_=t12, func=AF.Exp), reads=["lamv"], writes=["lamv"])
        P.op("dve", lambda e, c0=c0: e.tensor_tensor(out=lamv[:, c0:c0 + 1], in0=lamv[:, c0 + 5:c0 + 6],
                                                    in1=lamv[:, c0 + 4:c0 + 5], op=ALU.subtract),
             reads=["lamv"], writes=["lamv"])
        P.op("dve", lambda e, c0=c0, li=lam_init: e.tensor_scalar(out=lamv[:, c0:c0 + 1], in0=lamv[:, c0:c0 + 1],
                                                                scalar1=-li, scalar2=None, op0=ALU.add),
             reads=["lamv"], writes=["lamv"])
        P.op("dve", lambda e, c0=c0, l=l, li=lam_init: e.tensor_scalar(out=lamv[:, c0 + 1:c0 + 2], in0=pcol("subln", l),
                                                                     scalar1=1.0 - li, scalar2=None, op0=ALU.mult),
             reads=["prm"], writes=["lamv"])
    P.barrier()

    def norm_phase(gname, l, final=False):
        S.reset()
        xs = [S.alloc([KC, 256], F32) for _ in range(2)]
        sq = [S.alloc([256], BF16) for _ in range(4)]
        rs = [S.alloc([256], F32) for _ in range(2)]
        ost = [S.alloc([256], F32) for _ in range(4)] if final else None
        pn = ps[6]
        nsq = 0
        for sbk in range(4):
            t0 = sbk * 256
            tb = sbk // 2
            X = xs[sbk % 2]
            xid = ("xs", sbk % 2)
            P.dma("sp", X, xres[:, :, t0:t0 + 256].rearrange("k p t -> p k t"),
                  reads=[("x", m, tb) for m in range(KC)], writes=[xid])
            for kc in range(KC):
                q = sq[nsq % 4]
                qid = ("sq", nsq % 4)
                nsq += 1
                P.op("act", lambda e, q=q, X=X, kc=kc: e.activation(out=q, in_=X[:, kc, :], func=AF.Square),
                     reads=[xid], writes=[qid])
                P.op("pe", lambda e, q=q, kc=kc: e.matmul(pn[:, 0:256], lhsT=ones_bf, rhs=q, start=(kc == 0), stop=(kc == KC - 1)),
                     reads=[qid, "cbf"], writes=["pn"])
            R_ = rs[sbk % 2]
            rid = ("rs", sbk % 2)
            P.op("dve", lambda e, R_=R_: e.tensor_scalar(out=R_, in0=pn[:, 0:256], scalar1=1.0 / D, scalar2=NORM_EPS,
                                                       op0=ALU.mult, op1=ALU.add), reads=["pn"], writes=[rid])
            P.op("act", lambda e, R_=R_: e.activation(out=R_, in_=R_, func=AF.Sqrt), reads=[rid], writes=[rid])
            P.op("dve", lambda e, R_=R_: e.reciprocal(out=R_, in_=R_), reads=[rid], writes=[rid])
            for kc in range(KC):
                gcol = pcol(gname, (l * 32 if gname != "g_fin" else 0) + kc)
                if not final:
                    P.op("dve", lambda e, X=X, kc=kc, gcol=gcol, R_=R_, t0=t0: e.scalar_tensor_tensor(
                        out=hT[:, kc, t0:t0 + 256], in0=X[:, kc, :], scalar=gcol, in1=R_, op0=ALU.mult, op1=ALU.mult),
                        reads=[xid, rid, "prm"], writes=[("hT", tb)])
                else:
                    o_ = ost[kc % 4]
                    oid = ("ost", kc % 4)
                    P.op("dve", lambda e, X=X, kc=kc, gcol=gcol, R_=R_, o_=o_: e.scalar_tensor_tensor(
                        out=o_, in0=X[:, kc, :], scalar=gcol, in1=R_, op0=ALU.mult, op1=ALU.mult),
                        reads=[xid, rid, "prm"], writes=[oid])
                    P.dma("sp", outT[kc * 128:(kc + 1) * 128, t0:t0 + 256], o_, reads=[oid], writes=[("out", kc, sbk)])
        P.barrier()

    class _View:
        def __init__(self, kind, j):
            self.kind, self.j = kind, j

        def __getitem__(self, idx):
            if isinstance(idx, tuple):
                h = idx[0]
                rest = idx[1:]
            else:
                h, rest = idx, None
            if self.j is None:
                a = xl[(self.kind, h)].ap()
            else:
                a = xa[(self.kind, h)][self.j * 128:(self.j + 1) * 128, :]
            if self.kind[0] == "v":
                a = a.rearrange("a (b d) -> (a b) d", d=128)
            if rest is not None:
                a = a[rest]
            return a

    def xl_view(kind):
        return _View(kind, None)

    def xa_view(kind, j):
        return _View(kind, j)

    def xchg_cc(kind, idx):
        u = (kind, idx)
        P.cc(lambda e, u=u: e.collective_compute("AllGather", ALU.bypass, replica_groups=[[0, 1, 2, 3], [4, 5, 6, 7]],
                                               ins=[xl[u].ap()], outs=[xa[u].ap()]),
             reads=[("xl", kind, idx, 0), ("xl", kind, idx, 1)], writes=[("xa", kind, idx)])

    def inproj_phase(l):
        S.reset()
        rope = S.alloc([4, T], F32)
        P.dma("sp", rope.rearrange("p a t -> p (a t)"), rope_in, writes=["rope"])
        qb16 = [S.alloc([512], BF16) for _ in range(3)]
        t1 = [S.alloc([512], F32) for _ in range(2)]
        t2 = [S.alloc([512], F32) for _ in range(2)]
        ob = [S.alloc([512], BF16) for _ in range(3)]
        vtok = [S.alloc([4, 128], BF16) for _ in range(2)]
        fst = [S.alloc([512], F32) for _ in range(3)]
        halo = S.alloc([HC], F32)
        P.op("dve", lambda e: e.memset(halo, 0.0), writes=["halo"])
        cnt = {"qb": 0, "t": 0, "ob": 0, "vt": 0, "fs": 0, "acc": 0, "rot": 0}
        pend = []

        def flush():
            while pend:
                f = pend.pop(0)
                if "tail" not in SKIP:
                    f()

        xlw = []
        for b in range(INPROJ_NB):
            wt, wid = w_get()
            wv = wt[:, 0:8192].rearrange("p (k c) -> p k c", c=256)
            for ci in range(2):
                m = 2 * gb_of("in", b) + ci
                kind, idx = seg_of(m)
                for tb in range(2):
                    acc = ps[cnt["acc"] % 4]
                    aid = ("ps", cnt["acc"] % 4)
                    cnt["acc"] += 1
                    for kc in range(KC):
                        P.op("pe", lambda e, acc=acc, wv=wv, kc=kc, ci=ci, tb=tb: e.matmul(
                            acc, lhsT=wv[:, kc, ci * 128:(ci + 1) * 128], rhs=hT[:, kc, tb * 512:(tb + 1) * 512],
                            start=(kc == 0), stop=(kc == KC - 1)), reads=wid + [("hT", tb)], writes=[aid])
                    flush()
                    if kind in ("qA", "kA", "qB", "kB"):
                        isA = kind[1] == "A"
                        Rm = RmA_bf if isA else RmB_bf
                        cosT = rope[:, 0 if isA else 2, tb * 512:(tb + 1) * 512]
                        sinT = rope[:, 1 if isA else 3, tb * 512:(tb + 1) * 512]
                        q16 = qb16[cnt["qb"] % 3]
                        q16id = ("qb16", cnt["qb"] % 3)
                        cnt["qb"] += 1
                        P.op("act", lambda e, q16=q16, acc=acc: e.activation(out=q16, in_=acc, func=AF.Copy),
                             reads=[aid], writes=[q16id])
                        T1 = t1[cnt["t"] % 2]
                        T2 = t2[cnt["t"] % 2]
                        tid = ("t12", cnt["t"] % 2)
                        cnt["t"] += 1
                        P.op("dve", lambda e, T1=T1, acc=acc, cosT=cosT: e.tensor_tensor(out=T1, in0=acc, in1=cosT, op=ALU.mult),
                             reads=[aid, "rope"], writes=[tid])
                        rot = ps[4 + cnt["rot"] % 2]
                        rotid = ("ps", 4 + cnt["rot"] % 2)
                        cnt["rot"] += 1
                        O = ob[cnt["ob"] % 3]
                        oid = ("ob", cnt["ob"] % 3)
                        cnt["ob"] += 1
                        if kind[0] == "q":
                            dst = (qsA if isA else qsB)[idx, :, tb * 512:(tb + 1) * 512]
                            did = (kind, idx, tb)
                        else:
                            dst = xl_view(kind)[idx, :, tb * 512:(tb + 1) * 512]
                            did = ("xl", kind, idx, tb)

                        def tail(Rm=Rm, q16=q16, q16id=q16id, rot=rot, rotid=rotid, T1=T1, T2=T2, tid=tid,
                                 sinT=sinT, O=O, oid=oid, dst=dst, did=did, kind=kind, idx=idx, tb=tb):
                            P.op("pe", lambda e: e.matmul(rot, lhsT=Rm, rhs=q16, start=True, stop=True),
                                 reads=[q16id, "cbf"], writes=[rotid])
                            P.op("dve", lambda e: e.tensor_tensor(out=T2, in0=rot, in1=sinT, op=ALU.mult),
                                 reads=[rotid, "rope"], writes=[(tid, 2)])
                            P.op("dve", lambda e: e.tensor_tensor(out=O, in0=T1, in1=T2, op=ALU.add),
                                 reads=[tid, (tid, 2)], writes=[oid])
                            P.dma("sp", dst, O, reads=[oid], writes=[did])
                            if kind[0] == "k" and tb == 1:
                                xchg_cc(kind, idx)
                        pend.append(tail)
                    elif kind in ("vA", "vB"):
                        q16 = qb16[cnt["qb"] % 3]
                        q16id = ("qb16", cnt["qb"] % 3)
                        cnt["qb"] += 1
                        P.op("act", lambda e, q16=q16, acc=acc: e.activation(out=q16, in_=acc, func=AF.Copy),
                             reads=[aid], writes=[q16id])
                        V = vtok[cnt["vt"] % 2]
                        vid = ("vtok", cnt["vt"] % 2)
                        cnt["vt"] += 1
                        dst = xl_view(kind)[idx].rearrange("(t p) d -> p t d", p=128)[:, 4 * tb:4 * tb + 4, :]
                        did = ("xl", kind, idx, tb)

                        def tailv(q16=q16, q16id=q16id, V=V, vid=vid, dst=dst, did=did, kind=kind, idx=idx, tb=tb):
                            for j in range(4):
                                P.op("pe", lambda e, j=j: e.transpose(psT[:, j * 128:(j + 1) * 128], q16[:, j * 128:(j + 1) * 128], ident_bf),
                                     reads=[q16id, "cbf"], writes=["psT"])
                            P.op("dve", lambda e: e.tensor_copy(out=V.rearrange("p a b -> p (a b)"), in_=psT[:, 0:512]),
                                 reads=["psT"], writes=[vid])
                            P.dma("sp", dst, V, reads=[vid], writes=[did])
                            if tb == 1:
                                xchg_cc(kind, idx)
                        pend.append(tailv)
                    else:
                        F_ = fst[cnt["fs"] % 3]
                        fid = ("fst", cnt["fs"] % 3)
                        cnt["fs"] += 1
                        P.op("act", lambda e, F_=F_, acc=acc: e.activation(out=F_, in_=acc, func=AF.Copy),
                             reads=[aid], writes=[fid])
                        P.dma("sp", fsc[kind][idx, :, tb * 512:(tb + 1) * 512], F_, reads=[fid], writes=[(kind, idx, tb)])
                        if tb == 1 and kind in ("u", "gc", "hd"):
                            if kind == "u":
                                hs = halo[:, idx * 16:(idx + 1) * 16]
                                src = F_[:, 496:512]
                            else:
                                o0 = 128 + (0 if kind == "gc" else 14) + idx * 2
                                hs = halo[:, o0:o0 + 2]
                                src = F_[:, 510:512]
                            P.op("dve", lambda e, hs=hs, src=src: e.tensor_copy(out=hs, in_=src), reads=[fid], writes=["halo"])
        flush()
        P.dma("sp", hl.ap(), halo, reads=["halo"], writes=["hl"])
        if "hl" not in SKIP:
          P.cc(lambda e: e.collective_compute("AllGather", ALU.bypass, replica_groups=[[0, 1, 2, 3], [4, 5, 6, 7]],
                                            ins=[hl.ap()], outs=[ha.ap()]), reads=["hl"], writes=["ha"])
        P.barrier()

    mixT = hT

    def cd_phase(l):
        S.reset()
        hall = S.alloc([4, HC], F32)
        hsel = S.alloc([HC], F32)
        P.dma("sp", hall, ha.ap().rearrange("(j p) c -> p j c", p=128), reads=["ha"], writes=["hall"])
        P.op("dve", lambda e: e.tensor_scalar(out=hsel, in0=hall[:, 0, :], scalar1=pcol("oh", 0), scalar2=None, op0=ALU.mult),
             reads=["hall", "prm"], writes=["hsel"])
        for j in range(1, 4):
            P.op("dve", lambda e, j=j: e.scalar_tensor_tensor(out=hsel, in0=hall[:, j, :], scalar=pcol("oh", j), in1=hsel,
                                                            op0=ALU.mult, op1=ALU.add), reads=["hall", "prm", "hsel"], writes=["hsel"])
        pwf = S.alloc([2048], F32)
        pwb = S.alloc([4, 2, 256], BF16)
        P.dma("sp", pwf, poolw_in[:, l * 2048:(l + 1) * 2048], writes=["pwf"])
        P.op("act", lambda e: e.activation(out=pwb.rearrange("p a b c -> p (a b c)"), in_=pwf, func=AF.Copy), reads=["pwf"], writes=["pwb"])
        pooled = S.alloc([8, T], BF16)
        ue = [S.alloc([16 + T], F32) for _ in range(2)]
        sa = [S.alloc([16 + T], F32) for _ in range(2)]
        sbb = [S.alloc([16 + T], F32) for _ in range(2)]
        for ch in range(8):
            g = ch // 2
            w = C_WIN[g]
            E = ue[ch % 2]
            eid = ("ue", ch % 2)
            A_, B_ = sa[ch % 2], sbb[ch % 2]
            P.dma("sp", E[:, 16:16 + T], fsc["u"][ch], reads=[("u", ch, 0), ("u", ch, 1)], writes=[eid])
            P.op("act", lambda e, E=E, ch=ch: e.activation(out=E[:, 0:16], in_=hsel[:, ch * 16:(ch + 1) * 16], func=AF.Copy),
                 reads=["hsel"], writes=[eid])
            N = 16 + T
            cur, curid, sh = E, eid, 1
            k = 0
            while sh < w:
                dst = A_ if k % 2 == 0 else B_
                dstid = ("sab", ch % 2, k % 2)
                P.op("dve", lambda e, dst=dst, cur=cur, sh=sh: e.tensor_tensor(out=dst[:, sh:N], in0=cur[:, sh:N], in1=cur[:, 0:N - sh], op=ALU.add),
                     reads=[curid, eid], writes=[dstid])
                cur, curid = dst, dstid
                sh *= 2
                k += 1
            P.op("dve", lambda e, cur=cur, E=E, ch=ch, w=w: e.scalar_tensor_tensor(
                out=pooled[:, ch, 16:T], in0=cur[:, 32:16 + T], scalar=1.0 / w, in1=E[:, 32:16 + T], op0=ALU.mult, op1=ALU.subtract),
                reads=[curid, eid], writes=[("pooled", ch)])
            io = PM["icnt"][0] + g * 16
            P.op("dve", lambda e, cur=cur, io=io: e.tensor_tensor(out=cur[:, 0:16], in0=cur[:, 16:32], in1=prm[:, io:io + 16], op=ALU.mult),
                 reads=[curid, "prm"], writes=[curid])
            P.op("dve", lambda e, cur=cur, E=E, ch=ch: e.tensor_tensor(out=pooled[:, ch, 0:16], in0=cur[:, 0:16], in1=E[:, 16:32], op=ALU.subtract),
                 reads=[curid, eid], writes=[("pooled", ch)])
        na = 0
        for g in range(4):
            for dc in range(2):
                for tb in range(2):
                    acc = ps[na % 4]
                    aid = ("ps", na % 4)
                    na += 1
                    for cc_ in range(2):
                        P.op("pe", lambda e, acc=acc, g=g, cc_=cc_, dc=dc, tb=tb: e.matmul(
                            acc, lhsT=pwb[:, g, cc_, dc * 128:(dc + 1) * 128], rhs=pooled[:, 2 * g + cc_, tb * 512:(tb + 1) * 512],
                            start=(cc_ == 0), stop=(cc_ == 1)), reads=["pwb", ("pooled", 2 * g), ("pooled", 2 * g + 1)], writes=[aid])
                    P.op("act", lambda e, acc=acc, g=g, dc=dc, tb=tb: e.activation(
                        out=mixT[:, 17 + 2 * g + dc, tb * 512:(tb + 1) * 512], in_=acc, func=AF.Copy, scale=pcol("pscale", l * 8 + 2 * g + dc)),
                        reads=[aid, "prm"], writes=[("hT", tb)])
        ce = [S.alloc([2 + T], F32) for _ in range(2)]
        he = [S.alloc([2 + T], F32) for _ in range(2)]
        gbt = [S.alloc([T], F32) for _ in range(2)]
        yy = [S.alloc([T], F32) for _ in range(2)]
        for ch in range(7):
            C_, H_, G_, Y_ = ce[ch % 2], he[ch % 2], gbt[ch % 2], yy[ch % 2]
            cid, hid, gid, yid = ("ce", ch % 2), ("he", ch % 2), ("gbt", ch % 2), ("yy", ch % 2)
            P.dma("sp", C_[:, 2:2 + T], fsc["gc"][ch], reads=[("gc", ch, 0), ("gc", ch, 1)], writes=[cid])
            P.dma("sp", H_[:, 2:2 + T], fsc["hd"][ch], reads=[("hd", ch, 0), ("hd", ch, 1)], writes=[hid])
            P.dma("sp", G_, fsc["gb"][ch], reads=[("gb", ch, 0), ("gb", ch, 1)], writes=[gid])
            P.op("act", lambda e, C_=C_, ch=ch: e.activation(out=C_[:, 0:2], in_=hsel[:, 128 + ch * 2:130 + ch * 2], func=AF.Copy),
                 reads=["hsel"], writes=[cid])
            P.op("act", lambda e, H_=H_, ch=ch: e.activation(out=H_[:, 0:2], in_=hsel[:, 142 + ch * 2:144 + ch * 2], func=AF.Copy),
                 reads=["hsel"], writes=[hid])
            P.op("dve", lambda e, C_=C_, H_=H_: e.tensor_tensor(out=C_, in0=C_, in1=H_, op=ALU.mult), reads=[cid, hid], writes=[cid])
            wo = PM["convw"][0] + l * 21
            P.op("dve", lambda e, C_=C_, Y_=Y_, wo=wo, ch=ch: e.tensor_scalar(out=Y_, in0=C_[:, 2:2 + T], scalar1=prm[:, wo + 14 + ch:wo + 15 + ch],
                                                                           scalar2=None, op0=ALU.mult), reads=[cid, "prm"], writes=[yid])
            P.op("dve", lambda e, C_=C_, Y_=Y_, wo=wo, ch=ch: e.scalar_tensor_tensor(out=Y_, in0=C_[:, 1:1 + T], scalar=prm[:, wo + 7 + ch:wo + 8 + ch],
                                                                                   in1=Y_, op0=ALU.mult, op1=ALU.add), reads=[cid, "prm", yid], writes=[yid])
            P.op("dve", lambda e, C_=C_, Y_=Y_, wo=wo, ch=ch: e.scalar_tensor_tensor(out=Y_, in0=C_[:, 0:T], scalar=prm[:, wo + ch:wo + ch + 1],
                                                                                   in1=Y_, op0=ALU.mult, op1=ALU.add), reads=[cid, "prm", yid], writes=[yid])
            P.op("dve", lambda e, Y_=Y_, G_=G_, ch=ch: e.tensor_tensor(out=mixT[:, 25 + ch, :], in0=Y_, in1=G_, op=ALU.mult),
                 reads=[yid, gid], writes=[("hT", 0), ("hT", 1)])
        P.barrier()

    def attn_phase(l):
        S.reset()
        rt = S.alloc([5, 512], F32)
        P.dma("sp", rt.rearrange("p a t -> p (a t)"), rtab_in, writes=["rt"])
        kt = [S.alloc([4, T], BF16) for _ in range(2)]
        vt = [S.alloc([32, 128], BF16) for _ in range(2)]
        qp = [S.alloc([2, T], BF16) for _ in range(2)]
        et = [S.alloc([512], BF16) for _ in range(4)]
        pt = [S.alloc([512], BF16) for _ in range(4)]
        mt = [S.alloc([512], BF16) for _ in range(2)]
        ft = [S.alloc([512], F32) for _ in range(6)]
        for i in range(2):
            P.op("dve", lambda e, i=i: e.memset(qp[i].rearrange("p a t -> p (a t)"), 0.0), writes=[("qp", i)])
        c = {"e": 0, "p": 0, "s": 0, "m": 0}
        thA0 = PM["thA"][0]
        hiB0 = PM["hiB"][0]
        c0 = 8 * l
        neglam = lamv[:, c0:c0 + 1]
        sublnw = lamv[:, c0 + 1:c0 + 2]

        def load_kv(kindk, kindv, h, slot):
            K_, V_ = kt[slot], vt[slot]
            for j in range(4):
                P.dma("sp", K_[:, j, :], xa_view(kindk, j)[h], reads=[("xa", kindk, h)], writes=[("kt", slot)])
                P.dma("sp", V_[:, 8 * j:8 * j + 8, :], xa_view(kindv, j)[h].rearrange("(t p) d -> p t d", p=128),
                      reads=[("xa", kindv, h)], writes=[("vt", slot)])
            return K_, V_

        for h in range(8):
            slot = h % 2
            K_, V_ = load_kv("kA", "vA", h, slot)
            Q_ = qp[slot]
            P.dma("sp", Q_[0:64, 0, :], qsA[h, 0:64, :], reads=[("qA", h, 0), ("qA", h, 1)], writes=[("qp", slot)])
            P.dma("sp", Q_[64:128, 1, :], qsA[h, 64:128, :], reads=[("qA", h, 0), ("qA", h, 1)], writes=[("qp", slot)])
            for qb_ in range(2):
                accs = [ps[2], ps[3], ps[4], ps[5]]
                stb = [[0, 1], [6, 7]]

                def a_st(KT, K_=K_, Q_=Q_, qb_=qb_, slot=slot):
                    r = []
                    for mp in range(2):
                        bk = stb[KT % 2][mp]
                        st, sid = ps[bk], ("ps", bk)
                        P.op("pe", lambda e, st=st, KT=KT, mp=mp: e.matmul(
                            st, lhsT=K_[:, KT // 8, (KT % 8) * 128:(KT % 8 + 1) * 128], rhs=Q_[:, mp, qb_ * 512:(qb_ + 1) * 512],
                            start=True, stop=True), reads=[("kt", slot), ("qp", slot)], writes=[sid])
                        r.append((st, sid))
                    return r

                sts = a_st(0)
                for KT in range(32):
                    nxt = a_st(KT + 1) if KT + 1 < 32 else None
                    pts = []
                    for mp in range(2):
                        st, sid = sts[mp]
                        E_ = et[c["e"] % 4]
                        eid = ("et", c["e"] % 4)
                        c["e"] += 1
                        P.op("act", lambda e, E_=E_, st=st: e.activation(out=E_, in_=st, func=AF.Exp, scale=0.125), reads=[sid], writes=[eid])
                        P_ = pt[c["p"] % 4]
                        pid = ("pt", c["p"] % 4)
                        c["p"] += 1
                        P.op("dve", lambda e, P_=P_, E_=E_, KT=KT, qb_=qb_: e.scalar_tensor_tensor(
                            out=P_, in0=rt[:, 0, :], scalar=prm[:, thA0 + qb_ * 32 + KT:thA0 + qb_ * 32 + KT + 1], in1=E_,
                            op0=ALU.is_ge, op1=ALU.mult), reads=["rt", "prm", eid], writes=[pid])
                        pts.append((P_, pid))
                    for mp in range(2):
                        P_, pid = pts[mp]
                        P.op("pe", lambda e, P_=P_, V_=V_, KT=KT, mp=mp: e.matmul(accs[mp], lhsT=V_[:, KT, :], rhs=P_, start=(KT == 0), stop=(KT == 31)),
                             reads=[pid, ("vt", slot)], writes=[("ps", 2 + mp)])
                        P.op("pe", lambda e, P_=P_, KT=KT, mp=mp: e.matmul(accs[2 + mp], lhsT=ones_bf, rhs=P_, start=(KT == 0), stop=(KT == 31)),
                             reads=[pid, "cbf"], writes=[("ps", 4 + mp)])
                    sts = nxt
                r0, r1, a_, b_, d_ = ft[0], ft[1], ft[2], ft[3], ft[4]
                P.op("dve", lambda e: e.reciprocal(out=r0, in_=accs[2]), reads=[("ps", 4)], writes=[("ft", 0)])
                P.op("dve", lambda e: e.reciprocal(out=r1, in_=accs[3]), reads=[("ps", 5)], writes=[("ft", 1)])
                P.op("dve", lambda e: e.tensor_tensor(out=a_, in0=accs[0], in1=r0, op=ALU.mult), reads=[("ps", 2), ("ft", 0)], writes=[("ft", 2)])
                P.op("dve", lambda e: e.tensor_tensor(out=b_, in0=accs[1], in1=r1, op=ALU.mult), reads=[("ps", 3), ("ft", 1)], writes=[("ft", 3)])
                P.op("dve", lambda e: e.scalar_tensor_tensor(out=d_, in0=b_, scalar=neglam, in1=a_, op0=ALU.mult, op1=ALU.add),
                     reads=[("ft", 2), ("ft", 3), "lamv"], writes=[("ft", 4)])
                M_ = mt[c["m"] % 2]
                mid = ("mt", c["m"] % 2)
                c["m"] += 1
                P.op("act", lambda e, M_=M_: e.activation(out=M_, in_=d_, func=AF.Square), reads=[("ft", 4)], writes=[mid])
                P.op("pe", lambda e, M_=M_: e.matmul(ps[6], lhsT=ones_bf, rhs=M_, start=True, stop=True), reads=[mid, "cbf"], writes=[("ps", 6)])
                rs_ = ft[5]
                P.op("dve", lambda e: e.tensor_scalar(out=rs_, in0=ps[6], scalar1=1.0 / 128, scalar2=DIFF_EPS, op0=ALU.mult, op1=ALU.add),
                     reads=[("ps", 6)], writes=[("ft", 5)])
                P.op("act", lambda e: e.activation(out=rs_, in_=rs_, func=AF.Sqrt), reads=[("ft", 5)], writes=[("ft", 5)])
                P.op("dve", lambda e: e.reciprocal(out=rs_, in_=rs_), reads=[("ft", 5)], writes=[("ft", 5)])
                P.op("dve", lambda e, h=h, qb_=qb_: e.scalar_tensor_tensor(out=mixT[:, h, qb_ * 512:(qb_ + 1) * 512], in0=d_, scalar=sublnw, in1=rs_,
                                                                        op0=ALU.mult, op1=ALU.mult),
                     reads=[("ft", 4), ("ft", 5), "lamv"], writes=[("hT", qb_)])
        ut = [S.alloc([2, 512], F32) for _ in range(3)]
        stt = [S.alloc([2, 512], F32) for _ in range(3)]
        nh = 0
        for hh in range(3):
            for g in range(3):
                head = g * 3 + hh
                slot = nh % 2
                nh += 1
                K_, V_ = load_kv("kB", "vB", head, slot)
                Q_ = qp[slot]
                P.dma("sp", Q_[:, 0, :], qsB[head], reads=[("qB", head, 0), ("qB", head, 1)], writes=[("qp", slot)])
                ti = 0 if g == 0 else (1 + 2 * (g - 1))
                tu = 0 if g == 0 else (2 + 2 * (g - 1))
                for qb_ in range(2):
                    bbk = [0, 1, 6, 7]

                    def b_st(KT, K_=K_, Q_=Q_, qb_=qb_, slot=slot):
                        bk = bbk[KT % 4]
                        st, sid = ps[bk], ("ps", bk)
                        P.op("pe", lambda e, st=st, KT=KT: e.matmul(
                            st, lhsT=K_[:, KT // 8, (KT % 8) * 128:(KT % 8 + 1) * 128], rhs=Q_[:, 0, qb_ * 512:(qb_ + 1) * 512],
                            start=True, stop=True), reads=[("kt", slot), ("qp", slot)], writes=[sid])
                        return st, sid

                    LA = 3
                    stq = [b_st(k) for k in range(LA)]
                    for KT in range(32):
                        if KT + LA < 32:
                            stq.append(b_st(KT + LA))
                        st, sid = stq.pop(0)
                        E_ = et[c["e"] % 4]
                        eid = ("et", c["e"] % 4)
                        c["e"] += 1
                        P.op("act", lambda e, E_=E_, st=st: e.activation(out=E_, in_=st, func=AF.Exp, scale=128 ** -0.5), reads=[sid], writes=[eid])
                        P_ = pt[c["p"] % 4]
                        pid = ("pt", c["p"] % 4)
                        c["p"] += 1
                        ci_ = qb_ * 32 + KT
                        P.op("dve", lambda e, E_=E_, ti=ti, ci_=ci_: e.scalar_tensor_tensor(
                            out=E_, in0=rt[:, ti, :], scalar=prm[:, thA0 + ci_:thA0 + ci_ + 1], in1=E_, op0=ALU.is_ge, op1=ALU.mult),
                            reads=["rt", "prm", eid], writes=[eid])
                        P.op("dve", lambda e, P_=P_, E_=E_, tu=tu, ci_=ci_, g=g: e.scalar_tensor_tensor(
                            out=P_, in0=rt[:, tu, :], scalar=prm[:, hiB0 + g * 64 + ci_:hiB0 + g * 64 + ci_ + 1], in1=E_, op0=ALU.is_le, op1=ALU.mult),
                            reads=["rt", "prm", eid], writes=[pid])
                        P.op("pe", lambda e, P_=P_, V_=V_, KT=KT: e.matmul(ps[2], lhsT=V_[:, KT, :], rhs=P_, start=(KT == 0), stop=(KT == 31)),
                             reads=[pid, ("vt", slot)], writes=[("ps", 2)])
                        P.op("pe", lambda e, P_=P_, KT=KT: e.matmul(ps[3], lhsT=ones_bf, rhs=P_, start=(KT == 0), stop=(KT == 31)),
                             reads=[pid, "cbf"], writes=[("ps", 3)])
                    P.op("act", lambda e, g=g, qb_=qb_: e.activation(out=ut[g][:, qb_, :], in_=ps[2], func=AF.Copy), reads=[("ps", 2)], writes=[("ut", g)])
                    P.op("dve", lambda e, g=g, qb_=qb_: e.tensor_copy(out=stt[g][:, qb_, :], in_=ps[3]), reads=[("ps", 3)], writes=[("stt", g)])
            tot = stt[0]
            P.op("dve", lambda e: e.tensor_tensor(out=tot, in0=stt[0], in1=stt[1], op=ALU.add), reads=[("stt", 0), ("stt", 1)], writes=[("stt", 0)])
            P.op("dve", lambda e: e.tensor_tensor(out=tot, in0=tot, in1=stt[2], op=ALU.add), reads=[("stt", 0), ("stt", 2)], writes=[("stt", 0)])
            P.op("dve", lambda e: e.reciprocal(out=tot, in_=tot), reads=[("stt", 0)], writes=[("stt", 0)])
            for g in range(3):
                P.op("dve", lambda e, g=g, hh=hh: e.tensor_tensor(out=mixT[:, 8 + 3 * g + hh, :].rearrange("p (a b) -> p a b", a=2), in0=ut[g], in1=tot, op=ALU.mult),
                     reads=[("ut", g), ("stt", 0)], writes=[("hT", 0), ("hT", 1)])
        P.barrier()

    def resid_evac(acc, aid, m, tb, xo, xoid, cnts):
        P.op("dve", lambda e: e.tensor_tensor(out=xo, in0=acc, in1=xo, op=ALU.add), reads=[aid, xoid], writes=[xoid])
        P.dma("sp", xres[m, :, tb * 512:(tb + 1) * 512], xo, reads=[xoid], writes=[("x", m, tb)])

    def outproj_phase(l):
        S.reset()
        xo = [S.alloc([512], F32) for _ in range(4)]
        n = 0
        for b in range(16):
            wt, wid = w_get()
            wv = wt[:, 0:8192].rearrange("p (k c) -> p k c", c=256)
            for ci in range(2):
                m = 2 * gb_of("out", b) + ci
                for tb in range(2):
                    X_ = xo[n % 4]
                    xid = ("xo", n % 4)
                    acc = ps[n % 4]
                    aid = ("ps", n % 4)
                    n += 1
                    P.dma("sp", X_, xres[m, :, tb * 512:(tb + 1) * 512], reads=[("x", m, tb)], writes=[xid])
                    for kc in range(KC):
                        P.op("pe", lambda e, acc=acc, wv=wv, kc=kc, ci=ci, tb=tb: e.matmul(
                            acc, lhsT=wv[:, kc, ci * 128:(ci + 1) * 128], rhs=mixT[:, kc, tb * 512:(tb + 1) * 512],
                            start=(kc == 0), stop=(kc == KC - 1)), reads=wid + [("hT", tb)], writes=[aid])
                    resid_evac(acc, aid, m, tb, X_, xid, None)
        P.barrier()

    def mlp_phase(l):
        S.reset()
        actg = S.alloc([KCG, T], BF16)
        rl = [S.alloc([512], F32) for _ in range(2)]
        xo = [S.alloc([512], F32) for _ in range(4)]
        n = 0
        nr = 0
        for g in range(4):
            for b in range(NBU // 4):
                wt, wid = w_get()
                wv = wt[:, 0:8192].rearrange("p (k c) -> p k c", c=256)
                for ci in range(2):
                    fc = 2 * b + ci
                    for tb in range(2):
                        acc = ps[n % 4]
                        aid = ("ps", n % 4)
                        n += 1
                        for kc in range(KC):
                            P.op("pe", lambda e, acc=acc, wv=wv, kc=kc, ci=ci, tb=tb: e.matmul(
                                acc, lhsT=wv[:, kc, ci * 128:(ci + 1) * 128], rhs=hT[:, kc, tb * 512:(tb + 1) * 512],
                                start=(kc == 0), stop=(kc == KC - 1)), reads=wid + [("hT", tb)], writes=[aid])
                        R_ = rl[nr % 2]
                        rid = ("rl", nr % 2)
                        nr += 1
                        P.op("act", lambda e, R_=R_, acc=acc: e.activation(out=R_, in_=acc, func=AF.Relu), reads=[aid], writes=[rid])
                        P.op("dve", lambda e, R_=R_, fc=fc, tb=tb: e.tensor_tensor(out=actg[:, fc, tb * 512:(tb + 1) * 512], in0=R_, in1=R_, op=ALU.mult),
                             reads=[rid], writes=[("actg", tb)])
            for cb in range(16):
                wt, wid = w_get()
                wv = wt[:, 0:KCG * 256].rearrange("p (k c) -> p k c", c=256)
                for ci in range(2):
                    m = 2 * cb + ci
                    for tb in range(2):
                        X_ = xo[n % 4]
                        xid = ("xo", n % 4)
                        acc = ps[n % 4]
                        aid = ("ps", n % 4)
                        n += 1
                        P.dma("sp", X_, xres[m, :, tb * 512:(tb + 1) * 512], reads=[("x", m, tb)], writes=[xid])
                        for fc in range(KCG):
                            P.op("pe", lambda e, acc=acc, wv=wv, fc=fc, ci=ci, tb=tb: e.matmul(
                                acc, lhsT=wv[:, fc, ci * 128:(ci + 1) * 128], rhs=actg[:, fc, tb * 512:(tb + 1) * 512],
                                start=(fc == 0), stop=(fc == KCG - 1)), reads=wid + [("actg", tb)], writes=[aid])
                        resid_evac(acc, aid, m, tb, X_, xid, None)
        P.barrier()

    def dump(name, src_sb):
        if debug and name in dbg:
            P.barrier()
            P.dma("sp", dbg[name].rearrange("k p t -> p k t"), src_sb, reads=[], writes=[("dbg", name)])
            P.barrier()

    def program():
        st = 0

        def chk():
            nonlocal st
            st += 1
            return stop is not None and st > stop
        if chk():
            return
        for l in range(L):
            norm_phase("g_mix", l)
            if l == 0:
                dump("h", hT[:])
            if chk():
                return
            inproj_phase(l)
            if l == 0 and "bg" not in SKIP:
                def mlp_gather(l_, gs=(0, 1, 2, 3)):
                    if NBU // 4 >= 8:
                        q = NBU // 32
                        for g in gs:
                            cast_and_gather("up", l_, range(g * q, (g + 1) * q))
                            cast_and_gather("dn", l_, range(2 * g, 2 * g + 2))
                    elif gs[0] == 0:
                        cast_and_gather("up", l_)
                        cast_and_gather("dn", l_)
                DEFER = (L > 1 and NBU // 4 >= 8)
                mlp_gather(0)
                for l2 in range(1, L):
                    cast_and_gather("in", l2)
                    cast_and_gather("out", l2)
                    if DEFER and l2 == L - 1:
                        mlp_gather(l2, (0, 1))
                    else:
                        mlp_gather(l2)
            if l == L - 1 and L > 1 and NBU // 4 >= 8:
                mlp_gather(l, (2, 3))
            if l == 0 and debug:
                P.barrier()
                P.dma("sp", dbg["qA"], qsA.ap(), reads=[], writes=[("dbg", "qA")])
                P.barrier()
            if chk():
                return
            cd_phase(l)
            if chk():
                return
            attn_phase(l)
            if l == 0:
                dump("mix", mixT[:])
            if chk():
                return
            outproj_phase(l)
            if l == 0 and debug:
                P.barrier()
                P.dma("sp", dbg["x1"], xres.ap(), reads=[], writes=[("dbg", "x1")])
                P.barrier()
            if chk():
                return
            norm_phase("g_mlp", l)
            if chk():
                return
            mlp_phase(l)
            if chk():
                return
        norm_phase("g_fin", 0, final=True)
    program()
    P.barrier()
    P.op("sp", lambda e: e.nop())
    P.emit(nc, es)
    es.close()
    return nc


def _rope_tab(pos, dim):
    inv = np.power(np.float32(10000.0), -(np.arange(0, dim, 2, dtype=np.float32) / np.float32(dim))).astype(np.float32)
    ang = (pos.astype(np.float32)[:, None] * inv[None, :]).astype(np.float32)
    return np.cos(ang.astype(np.float64)).astype(np.float32), np.sin(ang.astype(np.float64)).astype(np.float32)


def _tile_cols(w, nblk_local, r, kcn, blocks=None):
    K, N = w.shape
    if blocks is None:
        c0 = r * nblk_local * 256
        sub = w[:, c0:c0 + nblk_local * 256]
    else:
        sub = np.concatenate([w[:, gb * 256:(gb + 1) * 256] for gb in blocks], axis=1)
    t = sub.reshape(kcn, 128, nblk_local, 256).transpose(2, 1, 0, 3)
    return np.ascontiguousarray(t).reshape(nblk_local * 128, kcn * 256)


def prep_inputs(cfg, inp):
    L, KCG, NBU = cfg.L, cfg.KCG, cfg.NBU
    PM, NPRM = _prm_map(L)
    x = np.asarray(inp["x"], np.float32)
    w_in = np.asarray(inp["w_in"], np.float32)
    w_out = np.asarray(inp["w_out"], np.float32)
    w_up = np.asarray(inp["w_up"], np.float32)
    w_dn = np.asarray(inp["w_down"], np.float32)
    maps = []
    kl = np.arange(128)[:, None]
    ql = np.arange(512)[None, :]
    R1 = (ql - kl).astype(np.float32)
    rt = [R1]
    for d in (4, 16):
        ok = ((ql - kl) % d) == 0
        rt.append(np.where(ok, R1, -1e9).astype(np.float32))
        rt.append(np.where(ok, R1, 1e9).astype(np.float32))
    rtab = np.concatenate(rt, axis=1)
    RmA = np.zeros((128, 128), np.float32)
    RmB = np.zeros((128, 128), np.float32)
    for m in range(128):
        if (m % 64) < 32:
            RmA[m + 32, m] = -1.0
        else:
            RmA[m - 32, m] = 1.0
        if m < 64:
            RmB[m + 64, m] = -1.0
        else:
            RmB[m - 64, m] = 1.0
    ident = np.eye(128, dtype=np.float32)
    poolw = np.asarray(inp["pool_w"], np.float32)
    pw = poolw.reshape(L, 4, 2, 128, 256).transpose(3, 0, 1, 2, 4).reshape(128, L * 2048)
    pw = np.ascontiguousarray(pw)

    def put(prm, name, arr):
        o, n = PM[name]
        arr = np.asarray(arr, np.float32)
        if arr.ndim == 1:
            assert arr.shape[0] == n, (name, arr.shape, n)
            prm[:, o:o + n] = arr[None, :]
        else:
            a2 = arr.reshape(128, -1)
            assert a2.shape[1] == n, (name, a2.shape, n)
            prm[:, o:o + n] = a2

    for c in range(NCORE):
        bch, r = c // 4, c % 4
        m = {}
        m["xT"] = np.ascontiguousarray(x[bch, r * T:(r + 1) * T, :].T)
        m["w_in_s"] = np.concatenate([_tile_cols(w_in[l], 5, c, 32) for l in range(L)], axis=0)
        m["w_out_s"] = np.concatenate([_tile_cols(w_out[l], 2, c, 32) for l in range(L)], axis=0)
        ublk = [up_local_to_global(NBU, c, bl) for bl in range(NBU // 8)]
        m["w_up_s"] = np.concatenate([_tile_cols(w_up[l], NBU // 8, c, 32, ublk) for l in range(L)], axis=0)
        dn = []
        for l in range(L):
            for g in range(4):
                for cbl in range(2):
                    cb = 2 * c + cbl
                    sub = w_dn[l][g * cfg.FG:(g + 1) * cfg.FG, cb * 256:(cb + 1) * 256]
                    dn.append(np.ascontiguousarray(sub.reshape(KCG, 128, 256).transpose(1, 0, 2)).reshape(128, KCG * 256))
        m["w_dn_s"] = np.concatenate(dn, axis=0)
        prm = np.zeros((128, NPRM), np.float32)
        put(prm, "g_mix", np.asarray(inp["norm_mix"], np.float32).reshape(L, 32, 128).transpose(2, 0, 1))
        put(prm, "g_mlp", np.asarray(inp["norm_mlp"], np.float32).reshape(L, 32, 128).transpose(2, 0, 1))
        put(prm, "g_fin", np.asarray(inp["norm_final"], np.float32).reshape(32, 128).T)
        put(prm, "subln", np.asarray(inp["diff_subln"], np.float32).T)
        put(prm, "pscale", np.asarray(inp["pool_scale"], np.float32).reshape(L, 8, 128).transpose(2, 0, 1))
        put(prm, "convw", np.asarray(inp["conv_w"], np.float32).reshape(L, 3, 7, 128).transpose(3, 0, 1, 2))
        put(prm, "lam", np.asarray(inp["diff_lambda"], np.float32).reshape(-1))
        th = np.zeros(64, np.float32)
        hi = np.zeros((3, 64), np.float32)
        for qb_ in range(2):
            for KT in range(32):
                dl = 128 * KT - (1024 * r + 512 * qb_)
                th[qb_ * 32 + KT] = dl
                for g in range(3):
                    hi[g, qb_ * 32 + KT] = dl + 128 * B_DIL[g]
        put(prm, "thA", th)
        put(prm, "hiB", hi.reshape(-1))
        oh = np.zeros(4, np.float32)
        if r >= 1:
            oh[r - 1] = 1.0
        put(prm, "oh", oh)
        ic = np.zeros((4, 16), np.float32)
        for g in range(4):
            for t in range(16):
                ic[g, t] = 1.0 / (min(t + 1, C_WIN[g]) if r == 0 else C_WIN[g])
        put(prm, "icnt", ic.reshape(-1))
        put(prm, "RmA", RmA)
        put(prm, "RmB", RmB)
        put(prm, "ident", ident)
        m["prm"] = prm
        pos = np.arange(r * T, (r + 1) * T)
        ca, sa_ = _rope_tab(pos, 64)
        cb_, sb_ = _rope_tab(pos, 128)
        pA = np.arange(128) % 32
        pB = np.arange(128) % 64
        m["rope"] = np.ascontiguousarray(np.concatenate([ca.T[pA], sa_.T[pA], cb_.T[pB], sb_.T[pB]], axis=1))
        m["rtab"] = rtab
        m["poolw"] = pw
        maps.append(m)
    return maps


_NC_CACHE = {}


def run(cfg, inputs, debug=False, stop=None):
    key = (cfg.L, cfg.DFF, debug, stop)
    if key not in _NC_CACHE:
        _NC_CACHE[key] = build(cfg, debug, stop)
    nc = _NC_CACHE[key]
    maps = prep_inputs(cfg, inputs)
    res = run_bass_kernel_spmd(nc, maps, core_ids=list(range(NCORE)))
    x = np.asarray(inputs["x"])
    out = np.empty(x.shape, np.float32)
    for c in range(NCORE):
        out[c // 4, (c % 4) * T:(c % 4 + 1) * T, :] = res.results[c]["outT"].T
    return out, res


def kernel(**inputs):
    cfg = Cfg(depth=2, d_ff=16384)
    out, _ = run(cfg, inputs)
    return out
```
